# Optimizing a Trainium2 kernel written in Bass

```python
import math
import jax, jax.numpy as jnp
from jax import lax
import numpy as np

D_MODEL = 2048
BATCH = 8
SEQ = 2048
DEPTH = 1

GLA_HEADS = 4
GLA_DK = D_MODEL // 2 // GLA_HEADS
GLA_DV = D_MODEL // GLA_HEADS
GLA_GATE_RANK = 16
GLA_GATE_NORM = 16.0
GDN_HEADS = 16
GDN_DK = D_MODEL // GDN_HEADS
GDN_DV = D_MODEL // GDN_HEADS
CONV_K = 4
CHUNK = 64
N_BRANCH = 2
BRANCH_W = D_MODEL
NORM_EPS = 1e-6
LN_EPS = 1e-5
DEEPNORM_ALPHA = (2 * DEPTH) ** 0.25
DEEPNORM_BETA = (8 * DEPTH) ** -0.25

GLA_QK = GLA_HEADS * GLA_DK
GLA_V = GLA_HEADS * GLA_DV
GDN_QK = GDN_HEADS * GDN_DK
GDN_V = GDN_HEADS * GDN_DV
IN_SPLITS = (GLA_QK, GLA_QK, GLA_V, GLA_V, GLA_GATE_RANK,
             GDN_QK, GDN_QK, GDN_V, GDN_V, GDN_HEADS, GDN_HEADS,
             N_BRANCH * D_MODEL)
IN_IS_VALUE = (False, False, True, False, False,
               False, False, True, False, False, False,
               False)
IN_TOTAL = sum(IN_SPLITS)
CONV_CH = 2 * GDN_QK + GDN_V

kernel_name = "gla_gdn_gated_parallel_deepnorm_adaln"


def _split_offsets():
    offs, acc = [], 0
    for n in IN_SPLITS[:-1]:
        acc += n
        offs.append(acc)
    return offs


def _heads(t, n_heads):
    b, s, _ = t.shape
    return t.reshape(b, s, n_heads, -1).transpose(0, 2, 1, 3)


def _to_chunks(t):
    b, h, s, d = t.shape
    return t.reshape(b, h, s // CHUNK, CHUNK, d).transpose(2, 0, 1, 3, 4)


def _from_chunks(t):
    n, b, h, c, d = t.shape
    return t.transpose(1, 2, 0, 3, 4).reshape(b, h, n * c, d)


def _scalar_chunks(t):
    b, h, s = t.shape
    return t.reshape(b, h, s // CHUNK, CHUNK).transpose(2, 0, 1, 3)


def _l2norm(t):
    return t * lax.rsqrt(jnp.sum(t * t, axis=-1, keepdims=True) + NORM_EPS)


def _gated_rmsnorm(o, w, z):
    b, h, s, dv = o.shape
    o = o.transpose(0, 2, 1, 3)
    o = o * lax.rsqrt(jnp.mean(o * o, axis=-1, keepdims=True) + NORM_EPS) * w.astype(jnp.float32)
    o = o * jax.nn.silu(z.astype(jnp.float32)).reshape(b, s, h, dv)
    return o.reshape(b, s, h * dv)


def _layernorm(r, g, b):
    r = r.astype(jnp.float32)
    mu = jnp.mean(r, axis=-1, keepdims=True)
    var = jnp.mean(jnp.square(r - mu), axis=-1, keepdims=True)
    return (r - mu) * lax.rsqrt(var + LN_EPS) * g.astype(jnp.float32) + b.astype(jnp.float32)


def _causal_dwconv(u, w):
    k = w.shape[0]
    return lax.conv_general_dilated(u, w.astype(u.dtype)[:, None, :], window_strides=(1,),
                                    padding=[(k - 1, 0)], dimension_numbers=("NWC", "WIO", "NWC"),
                                    feature_group_count=u.shape[-1])


def _gla_chunked(q, k, v, log_a):
    b, h, s, dk = q.shape
    dv = v.shape[-1]
    q = q * dk ** -0.5
    causal = jnp.tril(jnp.ones((CHUNK, CHUNK), dtype=bool))

    def step(state, inp):
        q_c, k_c, v_c, g_c = inp
        cum = jnp.cumsum(g_c, axis=2)
        diff = cum[:, :, :, None, :] - cum[:, :, None, :, :]
        decay = jnp.exp(jnp.where(causal[:, :, None], diff, -jnp.inf))
        attn = jnp.einsum('bhtd,bhsd,bhtsd->bhts', q_c, k_c, decay)
        o = (jnp.einsum('bhtd,bhdv->bhtv', q_c * jnp.exp(cum), state)
             + jnp.einsum('bhts,bhsv->bhtv', attn, v_c))
        last = cum[:, :, -1:, :]
        k_dec = k_c * jnp.exp(last - cum)
        state = state * jnp.exp(last)[:, :, 0, :, None] + jnp.einsum('bhsd,bhsv->bhdv', k_dec, v_c)
        return state, o

    s0 = jnp.zeros((b, h, dk, dv), q.dtype)
    _, o = lax.scan(step, s0, (_to_chunks(q), _to_chunks(k), _to_chunks(v), _to_chunks(log_a)))
    return _from_chunks(o)


def _gdn_chunked(q, k, v, beta, g):
    dk = q.shape[-1]
    dv = v.shape[-1]
    q = q * dk ** -0.5
    qc, kc, vc = _to_chunks(q), _to_chunks(k), _to_chunks(v)
    bc, gc = _scalar_chunks(beta), _scalar_chunks(g)
    cum = jnp.cumsum(gc, axis=-1)
    causal = jnp.tril(jnp.ones((CHUNK, CHUNK), dtype=bool))
    strict = jnp.tril(jnp.ones((CHUNK, CHUNK), dtype=bool), -1)
    decay = jnp.exp(jnp.where(causal, cum[..., :, None] - cum[..., None, :], -jnp.inf))
    kb = kc * bc[..., None]
    vb = vc * bc[..., None]
    m = jnp.where(strict, jnp.einsum('nbhtd,nbhsd->nbhts', kb, kc) * decay, 0.0)
    lmat = m + jnp.eye(CHUNK, dtype=m.dtype)
    rhs = jnp.concatenate([vb, kb * jnp.exp(cum)[..., None]], axis=-1)
    sol = lax.linalg.triangular_solve(lmat, rhs, left_side=True, lower=True, unit_diagonal=True)
    u_c, w_c = sol[..., :dv], sol[..., dv:]
    qk = jnp.einsum('nbhtd,nbhsd->nbhts', qc, kc) * decay
    q_dec = qc * jnp.exp(cum)[..., None]
    k_dec = kc * jnp.exp(cum[..., -1:] - cum)[..., None]
    g_last = jnp.exp(cum[..., -1])

    def step(state, inp):
        qk_c, qd_c, kd_c, uu, ww, gl = inp
        v_new = uu - jnp.einsum('bhtd,bhdv->bhtv', ww, state)
        o = jnp.einsum('bhtd,bhdv->bhtv', qd_c, state) + jnp.einsum('bhts,bhsv->bhtv', qk_c, v_new)
        state = state * gl[..., None, None] + jnp.einsum('bhsd,bhsv->bhdv', kd_c, v_new)
        return state, o

    n, b, h = g_last.shape
    s0 = jnp.zeros((b, h, dk, dv), q.dtype)
    _, o = lax.scan(step, s0, (qk, q_dec, k_dec, u_c, w_c, g_last))
    return _from_chunks(o)


def _hybrid_layer(x, c, w_ada, b_ada, w_in, w_gk2, b_gk2, conv_w, a_log, dt_bias,
                  gla_norm_w, gdn_norm_w, w_branch, w_out, ln_g, ln_b):
    f32 = jnp.float32
    dt = x.dtype
    ada = jax.nn.silu(c) @ w_ada + b_ada
    shift, scale, gate = jnp.split(ada, 3, axis=-1)
    h = x * (1 + scale[:, None, :]) + shift[:, None, :]
    u = h @ w_in
    (gla_q, gla_k, gla_v, gla_g, gla_lr, gdn_q, gdn_k, gdn_v, gdn_z, gdn_b, gdn_a,
     merge_g) = jnp.split(u, _split_offsets(), axis=-1)

    log_a = jax.nn.log_sigmoid((gla_lr @ w_gk2 + b_gk2).astype(f32)) / GLA_GATE_NORM
    o_gla = _gla_chunked(_heads(gla_q.astype(f32), GLA_HEADS), _heads(gla_k.astype(f32), GLA_HEADS),
                         _heads(gla_v.astype(f32), GLA_HEADS), _heads(log_a, GLA_HEADS))
    y_gla = _gated_rmsnorm(o_gla, gla_norm_w, gla_g).astype(dt)

    qkv = jax.nn.silu(_causal_dwconv(jnp.concatenate([gdn_q, gdn_k, gdn_v], axis=-1), conv_w))
    qkv = qkv.astype(f32)
    q = _l2norm(_heads(qkv[..., :GDN_QK], GDN_HEADS))
    k = _l2norm(_heads(qkv[..., GDN_QK:2 * GDN_QK], GDN_HEADS))
    v = _heads(qkv[..., 2 * GDN_QK:], GDN_HEADS)
    beta = jax.nn.sigmoid(gdn_b.astype(f32)).transpose(0, 2, 1)
    g = (-jnp.exp(a_log.astype(f32))
         * jax.nn.softplus(gdn_a.astype(f32) + dt_bias.astype(f32))).transpose(0, 2, 1)
    o_gdn = _gdn_chunked(q, k, v, beta, g)
    y_gdn = _gated_rmsnorm(o_gdn, gdn_norm_w, gdn_z).astype(dt)

    g_a, g_b = jnp.split(jax.nn.sigmoid(merge_g), 2, axis=-1)
    merged = g_a * (y_gla @ w_branch[0]) + g_b * (y_gdn @ w_branch[1])
    out = merged @ w_out
    r = DEEPNORM_ALPHA * x + gate[:, None, :] * out
    return _layernorm(r, ln_g, ln_b).astype(dt)


def setup_inputs(seed: int = 0) -> dict:
    key = jax.random.key(seed)
    ks = jax.random.split(key, 18)
    f32 = jnp.float32

    def nrm(k, shape, fan_in):
        return jax.random.normal(k, shape, f32) * fan_in ** -0.5

    x = jax.random.normal(ks[0], (BATCH, SEQ, D_MODEL), f32)
    c = jax.random.normal(ks[1], (BATCH, D_MODEL), f32)
    w_ada = nrm(ks[2], (DEPTH, D_MODEL, 3 * D_MODEL), D_MODEL)
    b_ada = 0.02 * jax.random.normal(ks[3], (DEPTH, 3 * D_MODEL), f32)
    col_scale = jnp.concatenate([jnp.full((n,), DEEPNORM_BETA if isv else 1.0, f32)
                                 for n, isv in zip(IN_SPLITS, IN_IS_VALUE)])
    w_in = nrm(ks[4], (DEPTH, D_MODEL, IN_TOTAL), D_MODEL) * col_scale
    w_gk2 = nrm(ks[5], (DEPTH, GLA_GATE_RANK, GLA_QK), GLA_GATE_RANK)
    b_gk2 = 0.1 * jax.random.normal(ks[6], (DEPTH, GLA_QK), f32)
    conv_w = nrm(ks[7], (DEPTH, CONV_K, CONV_CH), CONV_K)
    a_log = jnp.log(jax.random.uniform(ks[8], (DEPTH, GDN_HEADS), f32, 1.0, 16.0))
    dt0 = jnp.exp(jax.random.uniform(ks[9], (DEPTH, GDN_HEADS), f32, math.log(1e-3), math.log(1e-1)))
    dt_bias = dt0 + jnp.log(-jnp.expm1(-dt0))
    gla_norm_w = 1.0 + 0.02 * jax.random.normal(ks[10], (DEPTH, GLA_DV), f32)
    gdn_norm_w = 1.0 + 0.02 * jax.random.normal(ks[11], (DEPTH, GDN_DV), f32)
    w_branch = nrm(ks[12], (DEPTH, N_BRANCH, BRANCH_W, D_MODEL), BRANCH_W) * DEEPNORM_BETA
    w_out = nrm(ks[13], (DEPTH, D_MODEL, D_MODEL), D_MODEL) * DEEPNORM_BETA
    ln_g = 1.0 + 0.02 * jax.random.normal(ks[14], (DEPTH, D_MODEL), f32)
    ln_b = 0.02 * jax.random.normal(ks[15], (DEPTH, D_MODEL), f32)
    return {"x": x, "c": c, "w_ada": w_ada, "b_ada": b_ada, "w_in": w_in, "w_gk2": w_gk2,
            "b_gk2": b_gk2, "conv_w": conv_w, "a_log": a_log, "dt_bias": dt_bias,
            "gla_norm_w": gla_norm_w, "gdn_norm_w": gdn_norm_w, "w_branch": w_branch,
            "w_out": w_out, "ln_g": ln_g, "ln_b": ln_b}


def reference(x, c, w_ada, b_ada, w_in, w_gk2, b_gk2, conv_w, a_log, dt_bias,
              gla_norm_w, gdn_norm_w, w_branch, w_out, ln_g, ln_b):
    for l in range(DEPTH):
        x = _hybrid_layer(x, c, w_ada[l], b_ada[l], w_in[l], w_gk2[l], b_gk2[l], conv_w[l],
                          a_log[l], dt_bias[l], gla_norm_w[l], gdn_norm_w[l], w_branch[l],
                          w_out[l], ln_g[l], ln_b[l])
    return x
```

```python
import numpy as np
import concourse.bass as bass
import concourse.mybir as mybir
from concourse.bass_utils import run_bass_kernel_spmd
from contextlib import ExitStack

F32 = mybir.dt.float32
BF16 = mybir.dt.bfloat16
AF = mybir.ActivationFunctionType
ALU = mybir.AluOpType
AX = mybir.AxisListType

ENGS = ("pe", "dve", "act", "pool", "sp")

D = 2048
SEQ = 2048
NT = 16
IN_TOTAL = 18480
C_GLQ, C_GLK, C_GLV, C_GLG, C_LR = 0, 1024, 2048, 4096, 6144
C_GDQ, C_GDK, C_GDV, C_GDZ, C_GDB, C_GDA, C_MG = 6160, 8208, 10256, 12304, 14352, 14368, 14384
ALPHA = 2.0 ** 0.25
NORM_EPS = 1e-6
LN_EPS = 1e-5

K_ID, K_TRIU, K_MS, K_ONES, K_NEG, K_NEGMT = 0, 128, 256, 384, 512, 1408
NCF = 1536


class Sched:
    def __init__(self, nc, es):
        self.nc = nc
        self.es = es
        self.q = {e: [] for e in ENGS}
        self.ops = []
        self.lastw = {}
        self.readers = {}
        self.seen = {e: {} for e in ENGS}
        self.chan_ops = {}

    def _add(self, eng, fn, reads, writes, chan, extra_deps=(), guard=()):
        oid = len(self.ops)
        deps = set(extra_deps)
        for k in guard:
            w = self.lastw.get(k)
            if w is not None:
                deps.add(w)
            deps.update(self.readers.get(k, ()))
        for k in reads:
            w = self.lastw.get(k)
            if w is not None:
                deps.add(w)
        for k in writes:
            w = self.lastw.get(k)
            if w is not None:
                deps.add(w)
            for r in self.readers.get(k, ()):
                o = self.ops[r]
                if o["chan"] == eng and chan == eng:
                    continue
                deps.add(r)
        waits = {}
        for d in deps:
            o = self.ops[d]
            c = o["chan"]
            if c == eng and eng == "pe":
                continue
            if self.seen[eng].get(c, -1) >= o["pos"]:
                continue
            waits[c] = max(waits.get(c, -1), o["pos"])
        for c, p in waits.items():
            self.seen[eng][c] = p
            self.ops[self.chan_ops[c][p]]["signal"] = True
        lst = self.chan_ops.setdefault(chan, [])
        op = dict(id=oid, eng=eng, fn=fn, chan=chan, pos=len(lst), waits=waits,
                  signal=isinstance(chan, tuple))
        lst.append(oid)
        self.ops.append(op)
        self.q[eng].append(op)
        for k in reads:
            self.readers.setdefault(k, []).append(oid)
        for k in writes:
            self.lastw[k] = oid
            self.readers[k] = []
        return oid

    def op(self, eng, fn, reads=(), writes=()):
        return self._add(eng, fn, tuple(reads), tuple(writes), eng)

    def dma(self, eng, out, in_, reads=(), writes=(), grp=None, guard=(), **kw):
        fn = lambda e, out=out, in_=in_, kw=kw: e.dma_start(out=out, in_=in_, **kw)
        return self._add(eng, fn, tuple(reads), tuple(writes), ("dma", grp), guard=tuple(guard))

    def barrier(self):
        def pers(c):
            return isinstance(c, tuple) and str(c[1]).startswith("cv")
        last = [lst[-1] for c, lst in self.chan_ops.items() if lst and not pers(c)]
        for e in ENGS:
            self._add(e, lambda en: en.nop(), (), (), e, extra_deps=last)
        self.lastw = {k: v for k, v in self.lastw.items() if isinstance(k, str) and k.startswith("cv")}
        self.readers = {}

    def emit(self, final_wait_keys=()):
        nc = self.nc
        fw = {}
        for k in final_wait_keys:
            w = self.lastw.get(k)
            if w is not None:
                o = self.ops[w]
                fw[o["chan"]] = max(fw.get(o["chan"], -1), o["pos"])
                o["signal"] = True
        sems = {}
        vals = {}
        for c, lst in self.chan_ops.items():
            n = 0
            step = 16 if isinstance(c, tuple) else 1
            anysig = False
            for oid in lst:
                o = self.ops[oid]
                if o["signal"]:
                    n += step
                    anysig = True
                vals[oid] = n
            if anysig:
                nm = "s_" + (c if isinstance(c, str) else "d_" + str(c[1]))
                sems[c] = self.es.enter_context(nc.semaphore(nm))
        self.nsem = len(sems)
        self.maxval = max(vals.values()) if vals else 0

        def run(ename, e):
            for o in self.q[ename]:
                for c, p in o["waits"].items():
                    e.wait_ge(sems[c], vals[self.chan_ops[c][p]])
                ins = o["fn"](e)
                if o["signal"]:
                    ins.then_inc(sems[o["chan"]], 16 if isinstance(o["chan"], tuple) else 1)
            if ename == "sp":
                for c, p in fw.items():
                    e.wait_ge(sems[c], vals[self.chan_ops[c][p]])

        with nc.Block() as block:
            @block.tensor
            def _(e):
                run("pe", e)

            @block.vector
            def _(e):
                run("dve", e)

            @block.scalar
            def _(e):
                run("act", e)

            @block.gpsimd
            def _(e):
                run("pool", e)

            @block.sync
            def _(e):
                run("sp", e)


class Arena:
    def __init__(self, ap):
        self.ap = ap
        self.n = ap.shape[1]
        self.off = 0

    def alloc(self, free, dt=F32, parts=128):
        n = 1
        for f in free:
            n *= f
        words = n if dt == F32 else (n + 1) // 2
        assert self.off + words <= self.n, f"arena overflow {self.off}+{words}>{self.n}"
        v = self.ap[:, self.off:self.off + words]
        self.off += words
        if dt == BF16:
            v = v.bitcast(BF16)
            if n % 2:
                v = v[:, 0:n]
        if len(free) == 2:
            v = v.rearrange("p (a b) -> p a b", a=free[0])
        elif len(free) == 3:
            v = v.rearrange("p (a b c) -> p a b c", a=free[0], b=free[1])
        if parts < 128:
            v = v[0:parts]
        return v

    def reset(self):
        self.off = 0


def run_streams(gens):
    active = list(gens)
    while active:
        for g in list(active):
            try:
                next(g)
            except StopIteration:
                active.remove(g)


def host_consts():
    r = np.arange(128)
    c = np.zeros((128, NCF), np.float32)
    c[:, K_ID:K_ID + 128] = np.eye(128)
    c[:, K_TRIU:K_TRIU + 128] = (r[:, None] <= r[None, :])
    c[:, K_MS:K_MS + 128] = (r[:, None] > r[None, :])
    c[:, K_ONES:K_ONES + 128] = 1.0
    for j in range(7):
        b = 1 << j
        t = r[:, None]
        s = r[None, :]
        m = ((t // (2 * b)) == (s // (2 * b))) & ((t % (2 * b)) >= b) & ((s % (2 * b)) < b)
        c[:, K_NEG + j * 128:K_NEG + (j + 1) * 128] = -m.astype(np.float32)
    c[:, K_NEGMT:K_NEGMT + 128] = -100.0 * (r[None, :] <= r[:, None])
    return c


def build(stage=9, debug=False):
    nc = bass.Bass("TRN2", target_bir_lowering=False)

    def din(name, shape, dt=F32):
        return nc.dram_tensor(name, list(shape), dt, kind="ExternalInput").ap()

    x = din("x", [SEQ, D])
    cvec = din("c", [D])
    w_ada = din("w_ada", [D, 3 * D])
    b_ada = din("b_ada", [3 * D])
    w_in = din("w_in", [D, IN_TOTAL])
    w_gk2 = din("w_gk2", [16, 1024])
    b_gk2 = din("b_gk2", [1024])
    conv_w = din("conv_w", [4, 6144])
    a_log = din("a_log", [16])
    dt_bias = din("dt_bias", [16])
    gla_nw = din("gla_norm_w", [512])
    gdn_nw = din("gdn_norm_w", [128])
    w_br = din("w_branch", [2, D, D])
    w_out = din("w_out", [D, D])
    ln_g = din("ln_g", [D])
    ln_b = din("ln_b", [D])
    cst_d = din("cst", [128, NCF])
    out = nc.dram_tensor("out", [SEQ, D], F32, kind="ExternalOutput").ap()
    skind = "ExternalOutput" if debug else "Internal"
    yTa = nc.dram_tensor("yTa", [D, SEQ], BF16, kind=skind).ap()
    yTb = nc.dram_tensor("yTb", [D, SEQ], BF16, kind=skind).ap()
    gate_d = nc.dram_tensor("gate_d", [1, D], F32, kind="Internal").ap()
    wsrc = {"ada": w_ada, "in": w_in, "br0": w_br[0], "br1": w_br[1], "out": w_out}
    wbf = {k: nc.dram_tensor("wbf_" + k, list(v.shape), BF16, kind="Internal").ap() for k, v in wsrc.items()}
    CVB = 2048
    dbg = nc.dram_tensor("dbg", [8, 128, 512], F32, kind="ExternalOutput").ap() if debug else None

    es = ExitStack()
    with es:
        S = Sched(nc, es)

        def sbt(name, shape, dt=F32):
            return es.enter_context(nc.sbuf_tensor(name, shape, dt))

        hT = sbt("hT", [128, 16, SEQ], BF16)
        slab = [sbt(f"slab{i}", [128, 16, 512], BF16) for i in range(2)]
        cst = sbt("cstf", [128, NCF], F32)
        identb = sbt("identb", [128, 128], BF16)
        onesb = sbt("onesb", [128, 128], BF16)
        negmt4 = sbt("negmt4", [128, 4, 128], BF16)
        i4 = sbt("i4", [128, 4, 128], BF16)
        adaT = sbt("adaT", [128, 32], F32)
        ARW = 26200
        arena_t = sbt("arena", [128, ARW], F32)
        AR = Arena(arena_t[:])
        PA = es.enter_context(nc.psum_tensor("PA", [128, 2048], F32))
        PB = es.enter_context(nc.psum_tensor("PB", [128, 2048], F32))
        banks = [(PA[:, i * 512:(i + 1) * 512], f"PA{i}") for i in range(4)] + \
                [(PB[:, i * 512:(i + 1) * 512], f"PB{i}") for i in range(4)]
        bank_i = [0]

        bank_pools = {"P": [0, 1], "B": [2, 3, 4, 5], "C": [6, 7], "G": [2, 3, 4, 5, 6, 7]}
        pool_i = {"P": 0, "B": 0, "C": 0, "G": 0}

        def nb(pool=None):
            if pool is None:
                b = banks[bank_i[0] % 8]
                bank_i[0] += 1
                return b
            lst = bank_pools[pool]
            b = banks[lst[pool_i[pool] % len(lst)]]
            pool_i[pool] += 1
            return b

        ident = cst[:, K_ID:K_ID + 128]
        triU = cst[:, K_TRIU:K_TRIU + 128]
        maskS = cst[:, K_MS:K_MS + 128]
        onesf = cst[:, K_ONES:K_ONES + 128]

        def mm(o, lhsT, rhs, start, stop, reads, writes):
            S.op("pe", lambda e: e.matmul(o, lhsT=lhsT, rhs=rhs, start=start, stop=stop), reads, writes)

        def tr(o, in_, idn, reads, writes):
            S.op("pe", lambda e: e.transpose(out=o, in_=in_, identity=idn), reads, writes)

        def act(o, in_, func, reads, writes, **kw):
            S.op("act", lambda e: e.activation(out=o, in_=in_, func=func, **kw), reads, writes)

        def tt(eng, o, in0, in1, op, reads, writes):
            S.op(eng, lambda e: e.tensor_tensor(out=o, in0=in0, in1=in1, op=op), reads, writes)

        def ts(eng, o, in0, s1, s2, op0, op1, reads, writes):
            if s2 is None:
                S.op(eng, lambda e: e.tensor_scalar(out=o, in0=in0, scalar1=s1, scalar2=None, op0=op0), reads, writes)
            else:
                S.op(eng, lambda e: e.tensor_scalar(out=o, in0=in0, scalar1=s1, scalar2=s2, op0=op0, op1=op1), reads, writes)

        def stt(eng, o, in0, sc, in1, op0, op1, reads, writes):
            S.op(eng, lambda e: e.scalar_tensor_tensor(out=o, in0=in0, scalar=sc, in1=in1, op0=op0, op1=op1), reads, writes)

        def cp(eng, o, in_, reads, writes):
            if eng == "act":
                S.op("act", lambda e: e.copy(out=o, in_=in_), reads, writes)
            else:
                S.op(eng, lambda e: e.tensor_copy(out=o, in_=in_), reads, writes)

        def memset(eng, o, val, writes):
            S.op(eng, lambda e: e.memset(o, val), (), writes)

        slab_n = [0]

        CVBN = {"ada": 2048, "in": 512, "br0": 2048, "br1": 2048, "out": 2048}
        cv_pending = []
        n_in = (IN_TOTAL + 511) // 512
        for blk_ in range(12, n_in):
            cv_pending.append(("in", blk_ * 512))
        for name_ in ("br0", "br1", "out"):
            for c0_ in range(0, wsrc[name_].shape[1], CVBN[name_]):
                cv_pending.append((name_, c0_))

        def convert_more(k):
            for _ in range(k):
                if not cv_pending:
                    return
                name, c0 = cv_pending.pop(0)
                n = min(CVBN[name], wsrc[name].shape[1] - c0)
                kk = f"cv_{name}_{c0 // CVBN[name]}"
                S.dma("pool", wbf[name][:, c0:c0 + n], wsrc[name][:, c0:c0 + n], writes=[kk], grp=kk)

        def load_slab(pieces, direct=False):
            b = slab_n[0] % 2
            slab_n[0] += 1
            off = 0
            keys = []
            for pi, (name, c0, n) in enumerate(pieces):
                k = f"slab{b}.{pi}"
                if direct:
                    S.dma("pool", slab[b][:, :, off:off + n],
                          wsrc[name][:, c0:c0 + n].rearrange("(kc p) n -> p kc n", p=128),
                          writes=[k], grp=k, guard=[f"slab{b}.{i}" for i in range(4)])
                    keys.append(k)
                    off += n
                    continue
                cvk = [f"cv_{name}_{i}" for i in range(c0 // CVBN[name], (c0 + n - 1) // CVBN[name] + 1)]
                S.dma("sp", slab[b][:, :, off:off + n],
                      wbf[name][:, c0:c0 + n].rearrange("(kc p) n -> p kc n", p=128),
                      reads=cvk, writes=[k], grp=k, guard=[f"slab{b}.{i}" for i in range(4)])
                keys.append(k)
                off += n
            return b, keys

        def slab_keys_all(b):
            return [f"slab{b}.{i}" for i in range(2)]

        def inproj_fm(b, skeys, coff, m, tb, width=512):
            (pv, pk) = nb()
            for kc in range(16):
                mm(pv[0:m, 0:width], slab[b][:, kc, coff:coff + m], hT[:, kc, tb * width:(tb + 1) * width],
                   kc == 0, kc == 15, list(skeys) + ["hT"], [pk])
            return pv, pk

        def inproj_tm(b, skeys, coff, n, tile, pv=None, pk=None, pcol=0):
            if pv is None:
                (pv, pk) = nb()
            for kc in range(16):
                mm(pv[:, pcol:pcol + n], hT[:, kc, tile * 128:(tile + 1) * 128], slab[b][:, kc, coff:coff + n],
                   kc == 0, kc == 15, list(skeys) + ["hT"], [pk])
            return pv, pk

        S.dma("sp", cst[:], cst_d, writes=["cst"], grp="cst")
        cp("dve", identb[:], ident, ["cst"], ["identb"])
        cp("dve", onesb[:], onesf, ["cst"], ["onesb"])
        cp("dve", negmt4[:], cst[:, K_NEGMT:K_NEGMT + 128].unsqueeze(1).broadcast_to([128, 4, 128]), ["cst"], ["negmt4"])
        cp("dve", i4[:], ident.unsqueeze(1).broadcast_to([128, 4, 128]), ["cst"], ["i4"])

        cs = AR.alloc([16])
        scs = AR.alloc([16])
        sc16 = AR.alloc([16], BF16)
        scB = AR.alloc([16, 128], BF16)
        badT = AR.alloc([32])
        gate_bc = AR.alloc([2048])
        cs16 = AR.alloc([128], F32, parts=16)
        b32 = AR.alloc([128], F32, parts=32)
        S.dma("sp", cs16, cvec.rearrange("(kc p) -> kc p", p=128), writes=["cs16"], grp="cs16")
        S.dma("sp", b32, b_ada[0:4096].rearrange("(c p) -> c p", p=128), writes=["b32"], grp="b32")
        S.dma("sp", gate_bc, b_ada[4096:6144].partition_broadcast(128), writes=["gate_bc"], grp="gate_bc")
        (px0, kx0) = nb()
        tr(px0[:, 0:16], cs16, cst[0:16, K_ID:K_ID + 16], ["cs16", "cst"], [kx0])
        tr(px0[:, 16:48], b32, cst[0:32, K_ID:K_ID + 32], ["b32", "cst"], [kx0])
        cp("dve", cs, px0[:, 0:16], [kx0], ["cs"])
        cp("dve", badT, px0[:, 16:48], [kx0], ["badT"])
        act(scs, cs, AF.Silu, ["cs"], ["scs"])
        cp("dve", sc16, scs, ["scs"], ["sc16"])
        cp("dve", scB, scs.unsqueeze(2).broadcast_to([128, 16, 128]), ["scs"], ["scB"])
        (pada, padak) = nb()

        def ada_slab(sl):
            b, keys = load_slab([("ada", sl * 512, 512)], direct=True)
            if sl < 8:
                for q in range(4):
                    col = sl * 4 + q
                    for kc in range(16):
                        mm(pada[:, col:col + 1], slab[b][:, kc, q * 128:(q + 1) * 128], sc16[:, kc:kc + 1],
                           kc == 0, kc == 15, keys + ["sc16"], [padak])
            else:
                (pv, pk) = nb()
                for kc in range(16):
                    mm(pv, scB[:, kc, :], slab[b][:, kc, :], kc == 0, kc == 15, keys + ["scB"], [pk])
                blk = gate_bc[:, (sl - 8) * 512:(sl - 7) * 512]
                tt("dve", blk, pv, blk, ALU.add, [pk, "gate_bc"], ["gate_bc"])

        xs = AR.alloc([4, 2048])
        S.dma("sp", xs, x[0:512, :].rearrange("(j p) d -> p j d", p=128), writes=["xs"], grp="xs")
        for sl in range(8):
            ada_slab(sl)
        tt("dve", adaT[:], pada[:, 0:32], badT, ALU.add, [padak, "badT"], ["adaT"])
        ts("dve", adaT[:, 16:32], adaT[:, 16:32], 1.0, None, ALU.add, None, ["adaT"], ["adaT"])
        for tb in range(4):
            if tb > 0:
                S.dma("sp", xs, x[tb * 512:(tb + 1) * 512, :].rearrange("(j p) d -> p j d", p=128), writes=["xs"], grp="xs")
            for dc in range(16):
                (pv, pk) = nb()
                for j in range(4):
                    tr(pv[:, j * 128:(j + 1) * 128], xs[:, j, dc * 128:(dc + 1) * 128], ident, ["xs", "cst"], [pk])
                act(hT[:, dc, tb * 512:(tb + 1) * 512], pv, AF.Identity, [pk, "adaT"], [("hT", tb, dc)],
                    bias=adaT[:, dc:dc + 1], scale=adaT[:, 16 + dc:17 + dc])
            ada_slab(8 + tb)
        S.dma("sp", gate_d, gate_bc[0:1, :], reads=["gate_bc"], writes=["gate_d"], grp="gate_d")
        convert_more(8)
        S.barrier()
        AR.reset()

        if stage >= 1:
            lrT = AR.alloc([2048], F32, parts=32)
            wg2 = AR.alloc([1024], F32, parts=32)
            glw = AR.alloc([512])
            qT = [AR.alloc([2, 1024], BF16) for _ in range(2)]
            kT = [AR.alloc([2, 1024], BF16) for _ in range(2)]
            vt = [AR.alloc([8, 512], BF16) for _ in range(2)]
            gw = [AR.alloc([8, 512], BF16) for _ in range(2)]
            yTs = AR.alloc([4, 512], BF16)
            S32 = AR.alloc([2, 512])
            Sbf = AR.alloc([2, 512], BF16)
            pp = [AR.alloc([256]) for _ in range(2)]
            eqk = [AR.alloc([2, 256]) for _ in range(2)]
            qtl = [AR.alloc([2, 128], BF16) for _ in range(2)]
            ktl = [AR.alloc([2, 128], BF16) for _ in range(2)]
            ktm = [AR.alloc([256], BF16) for _ in range(2)]
            attT = [AR.alloc([128], BF16) for _ in range(2)]
            dS = [AR.alloc([2, 512]) for _ in range(2)]
            ytile = AR.alloc([512], BF16)
            gsil = AR.alloc([512])
            small = AR.alloc([8])
            junk = AR.alloc([512])
            triUb = AR.alloc([128], BF16)
            cp("dve", triUb, triU, [], ["triUb"])
            memset("dve", lrT, 1.0, ["lrT"])
            memset("dve", wg2, 0.0, ["wg2"])
            S.dma("sp", wg2[0:16, :], w_gk2, reads=[], writes=["wg2"], grp="wg2a")
            S.dma("sp", wg2[16:17, :], b_gk2.rearrange("(o n) -> o n", o=1), reads=[], writes=["wg2"], grp="wg2b")
            S.dma("sp", glw, gla_nw.partition_broadcast(128), writes=["glw"], grp="glw")
            b, keys = load_slab([("in", C_LR, 16)], direct=True)
            for tb in range(4):
                pv, pk = inproj_fm(b, keys, 0, 16, tb)
                cp("dve", lrT[0:16, tb * 512:(tb + 1) * 512], pv[0:16, :], [pk, "lrT"], ["lrT"])
            def gla_pre(hd, st, half, tl):
                t = half * 8 + tl
                lsl = slice(tl * 128, (tl + 1) * 128)
                s2 = tl % 2
                Z = str(s2)
                tsl = slice(t * 128, (t + 1) * 128)
                (p1, k1) = nb("G")
                mm(p1[:, 0:256], lrT[0:32, tsl], wg2[0:32, hd * 256:(hd + 1) * 256], True, True, ["lrT", "wg2"], [k1])
                yield
                act(pp[s2], p1[:, 0:256], AF.Exp, [k1], ["pp" + Z], scale=-1.0)
                act(pp[s2], pp[s2], AF.Ln, ["pp" + Z], ["pp" + Z], bias=1.0)
                yield
                (p2, k2) = nb("G")
                for c2 in range(2):
                    mm(p2[:, c2 * 128:(c2 + 1) * 128], pp[s2][:, c2 * 128:(c2 + 1) * 128], triU, True, True, ["pp" + Z, "cst"], [k2])
                yield
                act(eqk[s2][:, 0, :], p2[:, 0:256], AF.Exp, [k2], ["eq" + Z], scale=-1.0 / 16.0)
                act(eqk[s2][:, 1, :], p2[:, 0:256], AF.Exp, [k2], ["ek" + Z], scale=1.0 / 16.0)
                yield
                for c2 in range(2):
                    stt("dve", qtl[s2][:, c2, :], qT[st][:, c2, lsl], 256.0 ** -0.5, eqk[s2][:, 0, c2 * 128:(c2 + 1) * 128],
                        ALU.mult, ALU.mult, [f"gqT{st}", "eq" + Z], ["qtl" + Z])
                    tt("dve", ktl[s2][:, c2, :], kT[st][:, c2, lsl], eqk[s2][:, 1, c2 * 128:(c2 + 1) * 128], ALU.mult,
                       [f"gkT{st}", "ek" + Z], ["ktl" + Z])
                yield
                (p3, k3) = nb("G")
                p3b = p3.bitcast(BF16)
                for c2 in range(2):
                    tr(p3b[:, c2 * 128:(c2 + 1) * 128], ktl[s2][:, c2, :], identb[:], ["ktl" + Z, "identb"], [k3])
                yield
                cp("dve", ktm[s2], p3b[:, 0:256], [k3], ["ktm" + Z])
                (p4, k4) = nb("G")
                for c2 in range(2):
                    mm(p4[:, 0:128], ktl[s2][:, c2, :], qtl[s2][:, c2, :], c2 == 0, c2 == 1, ["ktl" + Z, "qtl" + Z], [k4])
                yield
                tt("dve", attT[s2], p4[:, 0:128], triU, ALU.mult, [k4, "cst"], ["attT" + Z])
                p7s = []
                for c2 in range(2):
                    (p7, k7) = nb("G")
                    mm(p7, ktm[s2][:, c2 * 128:(c2 + 1) * 128], vt[st][:, tl, :], True, True, ["ktm" + Z, f"gvt{st}"], [k7])
                    p7s.append((p7, k7))
                yield
                for c2 in range(2):
                    (p7, k7) = p7s[c2]
                    el = eqk[s2][:, 0, c2 * 128 + 127:c2 * 128 + 128]
                    act(dS[s2][:, c2, :], p7, AF.Copy, [k7, "eq" + Z], ["dS" + Z], scale=el)

            def gla_rec(hd, st, half, tl):
                t = half * 8 + tl
                s2 = tl % 2
                Z = str(s2)
                (p5, k5) = nb("G")
                for c2 in range(2):
                    mm(p5, qtl[s2][:, c2, :], Sbf[:, c2, :], c2 == 0, False, ["qtl" + Z, "Sbf"], [k5])
                mm(p5, attT[s2], vt[st][:, tl, :], False, True, ["attT" + Z, f"gvt{st}"], [k5])
                yield
                for c2 in range(2):
                    el = eqk[s2][:, 0, c2 * 128 + 127:c2 * 128 + 128]
                    stt("dve", S32[:, c2, :], S32[:, c2, :], el, dS[s2][:, c2, :], ALU.mult, ALU.add, ["S32", "eq" + Z, "dS" + Z], ["S32"])
                    cp("dve", Sbf[:, c2, :], S32[:, c2, :], ["S32"], ["Sbf"])
                yield
                act(junk, p5, AF.Square, [k5], ["junk", "ssq"], accum_out=small[:, 0:1])
                act(small[:, 1:2], small[:, 0:1], AF.Ln, ["ssq"], ["lssq"], scale=1.0 / 512.0, bias=NORM_EPS)
                act(small[:, 2:3], small[:, 1:2], AF.Exp, ["lssq"], ["rstd"], scale=-0.5)
                yield
                stt("dve", ytile, p5, small[:, 2:3], gw[st][:, tl, :], ALU.mult, ALU.mult, [k5, "rstd", f"ggw{st}"], ["ytile"])
                yield
                (p6, k6) = nb("G")
                p6b = p6.bitcast(BF16)
                for c4 in range(4):
                    tr(p6b[:, c4 * 128:(c4 + 1) * 128], ytile[:, c4 * 128:(c4 + 1) * 128], identb[:], ["ytile", "identb"], [k6])
                yield
                cp("act", yTs[:, :, (t % 4) * 128:(t % 4 + 1) * 128],
                   p6b[:, 0:512].rearrange("p (a b) -> p a b", a=4), [k6], ["yTs"])
                if t % 4 == 3:
                    S.dma("sp", yTa[hd * 512:(hd + 1) * 512, (t // 4) * 512:(t // 4 + 1) * 512].rearrange("(a p) n -> p a n", p=128),
                          yTs, reads=["yTs"], writes=["yTa"], grp="yTa")


            gev = set()

            def gla_inproj():
                for u in range(8):
                    hd, half = divmod(u, 2)
                    st = u % 2
                    while u >= 2 and ("Cdone", u - 2) not in gev:
                        yield
                    b, keys = load_slab([("in", C_GLQ + hd * 256, 256), ("in", C_GLK + hd * 256, 256)], direct=True)
                    for tb2 in range(2):
                        for (dst, co, dk_) in ((qT, 0, "gqT"), (kT, 256, "gkT")):
                            for c2 in range(2):
                                (pv, pk) = nb("P")
                                for kc in range(16):
                                    mm(pv, slab[b][:, kc, co + c2 * 128:co + (c2 + 1) * 128],
                                       hT[:, kc, (half * 2 + tb2) * 512:(half * 2 + tb2 + 1) * 512],
                                       kc == 0, kc == 15, list(keys) + ["hT"], [pk])
                                    if kc % 4 == 3:
                                        yield
                                cp("act", dst[st][:, c2, tb2 * 512:(tb2 + 1) * 512], pv, [pk], [f"{dk_}{st}"])
                    b, keys = load_slab([("in", C_GLV + hd * 512, 512)], direct=True)
                    for tl in range(8):
                        (pv, pk) = nb("P")
                        for kc in range(16):
                            mm(pv, hT[:, kc, (half * 8 + tl) * 128:(half * 8 + tl + 1) * 128], slab[b][:, kc, 0:512],
                               kc == 0, kc == 15, list(keys) + ["hT"], [pk])
                            if kc % 4 == 3:
                                yield
                        cp("act", vt[st][:, tl, :], pv, [pk], [f"gvt{st}"])
                    b, keys = load_slab([("in", C_GLG + hd * 512, 512)], direct=True)
                    for tl in range(8):
                        (pv, pk) = nb("P")
                        for kc in range(16):
                            mm(pv, hT[:, kc, (half * 8 + tl) * 128:(half * 8 + tl + 1) * 128], slab[b][:, kc, 0:512],
                               kc == 0, kc == 15, list(keys) + ["hT"], [pk])
                            if kc % 4 == 3:
                                yield
                        act(gsil, pv, AF.Silu, [pk], ["gsil"])
                        tt("dve", gw[st][:, tl, :], gsil, glw, ALU.mult, ["gsil", "glw"], [f"ggw{st}"])
                    gev.add(("ready", u))

            def gla_consumer():
                for u in range(8):
                    hd, half = divmod(u, 2)
                    st = u % 2
                    while ("ready", u) not in gev:
                        yield
                    if half == 0:
                        memset("dve", S32, 0.0, ["S32"])
                        memset("dve", Sbf, 0.0, ["Sbf"])
                    for _ in gla_pre(hd, st, half, 0):
                        yield
                    for tl in range(8):
                        if tl % 2 == 1:
                            convert_more(2)
                        subs = ([gla_pre(hd, st, half, tl + 1)] if tl + 1 < 8 else []) + [gla_rec(hd, st, half, tl)]
                        while subs:
                            for g in list(subs):
                                try:
                                    next(g)
                                except StopIteration:
                                    subs.remove(g)
                            yield
                    gev.add(("Cdone", u))

            run_streams([gla_inproj(), gla_consumer()])
            convert_more(1000)
            S.barrier()
            AR.reset()

        if stage >= 2:
            HG = 4
            TH = 8
            HS = TH * 128
            cacc = AR.alloc([512])
            lt = AR.alloc([512])
            ba = cacc.rearrange("p (a b) -> p a b", a=16)
            betat = AR.alloc([16, 16])
            gt = AR.alloc([16, 16])
            negA = AR.alloc([16])
            dtb = AR.alloc([16])
            gdw = AR.alloc([128])
            cw = AR.alloc([192])
            cwr = lt[0:96, 0:256].rearrange("p (a b) -> p a b", a=2)
            S.dma("sp", negA, a_log.partition_broadcast(128), writes=["negA"], grp="negA")
            S.dma("sp", dtb, dt_bias.partition_broadcast(128), writes=["dtb"], grp="dtb")
            S.dma("sp", gdw, gdn_nw.partition_broadcast(128), writes=["gdw"], grp="gdw")
            cw2 = conv_w.rearrange("j (c p) -> (j c) p", p=128)
            (pxc, kxc) = nb()
            for i2 in range(2):
                S.dma("sp", cwr[:, i2, :], cw2[i2 * 96:(i2 + 1) * 96, :], writes=["lt"], grp=f"cwr{i2}")
                tr(pxc[:, i2 * 96:(i2 + 1) * 96], cwr[:, i2, :], cst[0:96, K_ID:K_ID + 96], ["lt", "cst"], [kxc])
            cp("dve", cw, pxc[:, 0:192], [kxc], ["cw"])
            act(negA, negA, AF.Exp, ["negA"], ["negA"])
            ts("dve", negA, negA, -1.0, None, ALU.mult, None, ["negA"], ["negA"])
            b, keys = load_slab([("in", C_GDB, 32)])
            (pv, pk) = nb()
            for t in range(NT):
                inproj_tm(b, keys, 0, 32, t, pv, pk, pcol=t * 32)
            cp("dve", ba, pv.rearrange("p (a b) -> p a b", a=16), [pk], ["cacc"])
            act(betat, ba[:, :, 0:16], AF.Exp, ["cacc"], ["betat"], scale=-1.0)
            ts("dve", betat, betat, 1.0, None, ALU.add, None, ["betat"], ["betat"])
            S.op("dve", lambda e: e.reciprocal(out=betat, in_=betat), ["betat"], ["betat"])
            tt("dve", gt, ba[:, :, 16:32], dtb.unsqueeze(1).broadcast_to([128, 16, 16]), ALU.add, ["cacc", "dtb"], ["gt"])
            act(gt, gt, AF.Exp, ["gt"], ["gt"])
            act(gt, gt, AF.Ln, ["gt"], ["gt"], bias=1.0)
            tt("dve", gt, gt, negA.unsqueeze(1).broadcast_to([128, 16, 16]), ALU.mult, ["gt", "negA"], ["gt"])

            NU = 16
            qTg = [AR.alloc([HG, 512], BF16) for _ in range(2)]
            kTg = [AR.alloc([HG, 512], BF16) for _ in range(2)]
            vtm = [AR.alloc([4, 512], BF16) for _ in range(2)]
            zw = [AR.alloc([4, 512], BF16) for _ in range(2)]
            upad = AR.alloc([3 + 512], BF16)
            halo = AR.alloc([12, 3], BF16)
            sact = AR.alloc([512], BF16)
            sqb = AR.alloc([512], BF16)
            zs = cacc
            vsb = AR.alloc([512], BF16)
            dg = AR.alloc([4, 128], BF16)
            yTs = [AR.alloc([4, 128], BF16) for _ in range(2)]
            S32 = AR.alloc([HG, 128])
            Sbf = AR.alloc([HG, 128], BF16)
            B2 = AR.alloc([HG, 128])
            WT = AR.alloc([HG, 128], BF16)
            WTb = AR.alloc([HG, 128], BF16)
            NI = 4
            ATl = [AR.alloc([HG, 128], BF16) for _ in range(NI)]
            Tl = [AR.alloc([HG, 128], BF16) for _ in range(NI)]
            NXl = [AR.alloc([HG, 128], BF16) for _ in range(NI)]
            TTl = [AR.alloc([HG, 128], BF16) for _ in range(NI)]
            qkmb = [AR.alloc([HG, 128], BF16) for _ in range(NI)]
            ktmp = [AR.alloc([HG, 128], BF16) for _ in range(NI)]
            Q2Tl = [[AR.alloc([HG, 128], BF16) for _ in range(NI)] for _ in range(2)]
            M2Tl = [[AR.alloc([HG, 128], BF16) for _ in range(NI)] for _ in range(2)]
            WTib = AR.alloc([HG, 128], BF16)
            tmpf = AR.alloc([HG, 128])
            r0 = AR.alloc([HG, 128], BF16)
            o1 = AR.alloc([HG, 128])
            osq = tmpf
            ygd = AR.alloc([HG, 128], BF16)
            cums = AR.alloc([4, 4])
            ecum = [AR.alloc([4, 4]) for _ in range(2)]
            elc = AR.alloc([4, 4])
            elcb = AR.alloc([4, 4])
            glast = [AR.alloc([4, 4]) for _ in range(2)]
            sm = AR.alloc([8])
            ev = set()
            bc4 = lambda a2: a2.unsqueeze(2).broadcast_to([128, HG, 128])
            fl = lambda a3: a3.rearrange("p a b -> p (a b)")
            r4 = lambda a2: a2.rearrange("p (a b) -> p a b", a=HG)

            def gdn_prep():
                for u in range(NU):
                    gi, qb = divmod(u, 4)
                    h0 = gi * HG
                    st = u % 2
                    while u >= 2 and ("Cdone", u - 2) not in ev:
                        yield
                    for kind, cbase in (("q", C_GDQ), ("k", C_GDK), ("v", C_GDV)):
                        b, keys = load_slab([("in", cbase + h0 * 128, 512)])
                        for j in range(HG):
                            cch = (cbase - C_GDQ) // 128 + h0 + j
                            hidx = ("qkv".index(kind)) * HG + j
                            if qb == 0:
                                memset("pool", upad[:, 0:3], 0.0, ["upad"])
                            else:
                                cp("pool", upad[:, 0:3], halo[:, hidx, :], ["halo"], ["upad"])
                            (pv, pk) = nb("P")
                            for kc in range(16):
                                mm(pv, slab[b][:, kc, j * 128:(j + 1) * 128], hT[:, kc, qb * 512:(qb + 1) * 512],
                                   kc == 0, kc == 15, list(keys) + ["hT"], [pk])
                                if kc % 8 == 7:
                                    yield
                            cp("act", upad[:, 3:515], pv, [pk], ["upad"])
                            if qb < 3:
                                cp("pool", halo[:, hidx, :], upad[:, 512:515], ["upad"], ["halo"])
                            for jj in range(4):
                                ts("dve", dg[:, jj, :], identb[:], cw[:, jj * 48 + cch:jj * 48 + cch + 1], None, ALU.mult, None,
                                   ["identb", "cw"], ["dg"])
                            yield
                            (pcv, kcv) = nb("P")
                            for jj in range(4):
                                mm(pcv, dg[:, jj, :], upad[:, jj:jj + 512], jj == 0, jj == 3, ["dg", "upad"], [kcv])
                            yield
                            if kind == "v":
                                act(vsb, pcv, AF.Silu, [kcv], ["vsb"])
                                yield
                                yield
                                (p1, k1) = nb("P")
                                p1b = p1.bitcast(BF16)
                                for tq in range(4):
                                    tr(p1b[:, tq * 128:(tq + 1) * 128], vsb[:, tq * 128:(tq + 1) * 128], identb[:], ["vsb", "identb"], [k1])
                                yield
                                cp("act", vtm[st][:, :, j * 128:(j + 1) * 128],
                                   p1b[:, 0:512].rearrange("p (a b) -> p a b", a=4), [k1], [f"vtm{st}"])
                            else:
                                act(sact, pcv, AF.Silu, [kcv], ["sact"])
                                act(sqb, sact, AF.Square, ["sact"], ["sqb"])
                                yield
                                yield
                                (p1, k1) = nb("P")
                                mm(p1, onesb[:], sqb, True, True, ["sqb", "onesb"], [k1])
                                yield
                                act(lt, p1, AF.Ln, [k1], ["lt"], bias=NORM_EPS)
                                if kind == "q":
                                    act(lt, lt, AF.Exp, ["lt"], ["lt"], scale=-0.5, bias=-0.5 * float(np.log(128.0)))
                                    tt("pool", qTg[st][:, j, :], sact, lt, ALU.mult, ["sact", "lt"], [f"qTg{st}"])
                                else:
                                    act(lt, lt, AF.Exp, ["lt"], ["lt"], scale=-0.5)
                                    tt("pool", kTg[st][:, j, :], sact, lt, ALU.mult, ["sact", "lt"], [f"kTg{st}"])
                            yield
                    b, keys = load_slab([("in", C_GDZ + h0 * 128, 512)])
                    for tl in range(4):
                        (pv, pk) = nb("P")
                        for kc in range(16):
                            mm(pv, hT[:, kc, (qb * 4 + tl) * 128:(qb * 4 + tl + 1) * 128], slab[b][:, kc, 0:512],
                               kc == 0, kc == 15, list(keys) + ["hT"], [pk])
                            if kc % 8 == 7:
                                yield
                        act(zs, pv, AF.Silu, [pk], ["cacc"])
                        tt("pool", r4(zw[st][:, tl, :]), r4(zs), gdw.unsqueeze(1).broadcast_to([128, HG, 128]), ALU.mult,
                           ["cacc", "gdw"], [f"zw{st}"])
                        yield
                    ev.add(("ready", u))

            def gdn_B():
                for u in range(NU):
                    gi, qb = divmod(u, 4)
                    h0 = gi * HG
                    st = u % 2
                    while ("ready", u) not in ev or (u >= 2 and ("Cdone", u - 2) not in ev):
                        yield
                    (pc, kc_) = nb("B")
                    for tl in range(4):
                        mm(pc[:, tl * 4:(tl + 1) * 4], triU, gt[:, qb * 4 + tl, h0:h0 + HG], True, True, ["gt", "cst"], [kc_])
                        mm(pc[:, 64 + tl * 4:64 + (tl + 1) * 4], onesf, gt[:, qb * 4 + tl, h0:h0 + HG], True, True, ["gt", "cst"], [kc_])
                    yield
                    cp("dve", cums, pc[:, 0:16].rearrange("p (a b) -> p a b", a=4), [kc_], ["cums"])
                    act(ecum[st], cums, AF.Exp, ["cums"], [f"ecum{st}"])
                    act(glast[st], pc[:, 64:80].rearrange("p (a b) -> p a b", a=4), AF.Exp, [kc_], [f"glast{st}"])
                    tt("dve", elc, pc[:, 64:80].rearrange("p (a b) -> p a b", a=4), cums, ALU.subtract, [kc_, "cums"], ["elc"])
                    act(elc, elc, AF.Exp, ["elc"], ["elc"])
                    tt("pool", elcb, elc, betat[:, qb * 4:(qb + 1) * 4, h0:h0 + HG], ALU.mult, ["elc", "betat"], ["elcb"])
                    for i in range(NI):
                        t = qb * 4 + i
                        tsl = slice(i * 128, (i + 1) * 128)
                        tt("dve", B2, triU.unsqueeze(1).broadcast_to([128, HG, 128]), bc4(gt[:, t, h0:h0 + HG]), ALU.mult,
                           ["gt", "cst"], ["B2"])
                        yield
                        (p1, k1) = nb("B")
                        mm(p1, maskS, fl(B2), True, False, ["B2", "cst"], [k1])
                        mm(p1, identb[:], fl(negmt4[:]), False, True, ["identb", "negmt4"], [k1])
                        act(fl(WT), p1, AF.Exp, [k1], ["WT"])
                        tt("dve", WTib, WT, i4[:], ALU.add, ["WT", "i4"], ["WTib"])
                        tt("dve", WTb, WT, bc4(betat[:, t, h0:h0 + HG]), ALU.mult, ["WT", "betat"], ["WTb"])
                        tt("dve", WTib, WTib, bc4(betat[:, t, h0:h0 + HG]), ALU.mult, ["WTib", "betat"], ["WTib"])
                        (p2, k2) = nb("B")
                        (p3, k3) = nb("B")
                        for j in range(HG):
                            mm(p2[:, j * 128:(j + 1) * 128], kTg[st][:, j, tsl], kTg[st][:, j, tsl], True, True, [f"kTg{st}"], [k2])
                        for j in range(HG):
                            mm(p3[:, j * 128:(j + 1) * 128], kTg[st][:, j, tsl], qTg[st][:, j, tsl], True, True, [f"kTg{st}", f"qTg{st}"], [k3])
                        yield
                        tt("dve", fl(ATl[i]), p2, fl(WTb), ALU.mult, [k2, "WTb"], [f"AT{i}"])
                        tt("dve", fl(qkmb[i]), p3, fl(WTib), ALU.mult, [k3, "WTib"], [f"qkmb{i}"])
                        (p4, k4) = nb("B")
                        p4b = p4.bitcast(BF16)
                        for j in range(HG):
                            tr(p4b[:, j * 128:(j + 1) * 128], kTg[st][:, j, tsl], identb[:], [f"kTg{st}", "identb"], [k4])
                        tt("dve", ktmp[i], r4(p4b[:, 0:512]), bc4(elcb[:, i, :]), ALU.mult, [k4, "elcb"], [f"ktmp{i}"])
                        yield
                    for lv in range(7):
                        nm = cst[:, K_NEG + lv * 128:K_NEG + (lv + 1) * 128].unsqueeze(1).broadcast_to([128, HG, 128])
                        pxs = []
                        for i in range(NI):
                            Tc, Tk = (i4, "i4") if lv == 0 else (Tl[i], f"T{i}")
                            (px, kx) = nb("B")
                            for j in range(HG):
                                mm(px[:, j * 128:(j + 1) * 128], ATl[i][:, j, :], Tc[:, j, :], True, True, [f"AT{i}", Tk], [kx])
                            pxs.append((px, kx))
                        yield
                        for i in range(NI):
                            (px, kx) = pxs[i]
                            tt("dve", NXl[i], r4(px), nm, ALU.mult, [kx, "cst"], [f"NX{i}"])
                        yield
                        pys = []
                        for i in range(NI):
                            Tc, Tk = (i4, "i4") if lv == 0 else (Tl[i], f"T{i}")
                            TTc, TTk = (i4, "i4") if lv == 0 else (TTl[i], f"TT{i}")
                            (py, ky) = nb("B")
                            for j in range(HG):
                                mm(py[:, j * 128:(j + 1) * 128], TTc[:, j, :], NXl[i][:, j, :], True, True, [TTk, f"NX{i}"], [ky])
                            pys.append((py, ky))
                        yield
                        pzs = []
                        for i in range(NI):
                            (py, ky) = pys[i]
                            Tc, Tk = (i4, "i4") if lv == 0 else (Tl[i], f"T{i}")
                            tt("dve", fl(Tl[i]), py, fl(Tc[:]), ALU.add, [ky, Tk], [f"T{i}"])
                        if lv < 6:
                            for i in range(NI):
                                TTc, TTk = (i4, "i4") if lv == 0 else (TTl[i], f"TT{i}")
                                (pz, kz) = nb("B")
                                for j in range(HG):
                                    mm(pz[:, j * 128:(j + 1) * 128], NXl[i][:, j, :], TTc[:, j, :], True, True, [f"NX{i}", TTk], [kz])
                                pzs.append((pz, kz))
                            yield
                            for i in range(NI):
                                (pz, kz) = pzs[i]
                                TTc, TTk = (i4, "i4") if lv == 0 else (TTl[i], f"TT{i}")
                                tt("dve", fl(TTl[i]), pz, fl(TTc[:]), ALU.add, [kz, TTk], [f"TT{i}"])
                        yield
                    pqs = []
                    for i in range(NI):
                        (pq, kq) = nb("B")
                        for j in range(HG):
                            mm(pq[:, j * 128:(j + 1) * 128], Tl[i][:, j, :], qkmb[i][:, j, :], True, True, [f"T{i}", f"qkmb{i}"], [kq])
                        pqs.append((pq, kq))
                    yield
                    pms = []
                    for i in range(NI):
                        (pq, kq) = pqs[i]
                        cp("act", fl(Q2Tl[st][i]), pq, [kq], [f"Q2T{st}{i}"])
                    for i in range(NI):
                        (pm, km) = nb("B")
                        for j in range(HG):
                            mm(pm[:, j * 128:(j + 1) * 128], Tl[i][:, j, :], ktmp[i][:, j, :], True, True, [f"T{i}", f"ktmp{i}"], [km])
                        pms.append((pm, km))
                    yield
                    for i in range(NI):
                        (pm, km) = pms[i]
                        cp("dve", fl(M2Tl[st][i]), pm, [km], [f"M2T{st}{i}"])
                    yield
                    ev.add(("Bdone", u))

            def gdn_C():
                for u in range(NU):
                    gi, qb = divmod(u, 4)
                    h0 = gi * HG
                    st = u % 2
                    while ("Bdone", u) not in ev:
                        yield
                    if qb == 0:
                        memset("pool", S32, 0.0, ["gS32"])
                        memset("pool", Sbf, 0.0, ["gSbf"])
                    yield
                    for i in range(NI):
                        tl = i
                        t = qb * 4 + i
                        tsl = slice(i * 128, (i + 1) * 128)
                        (p5, k5) = nb("C")
                        (p6, k6) = nb("C")
                        for j in range(HG):
                            mm(p5[:, j * 128:(j + 1) * 128], kTg[st][:, j, tsl], Sbf[:, j, :], True, True, [f"kTg{st}", "gSbf"], [k5])
                        for j in range(HG):
                            mm(p6[:, j * 128:(j + 1) * 128], qTg[st][:, j, tsl], Sbf[:, j, :], True, True, [f"qTg{st}", "gSbf"], [k6])
                        tt("dve", tmpf, r4(p5), bc4(ecum[st][:, tl, :]), ALU.mult, [k5, f"ecum{st}"], ["tmpf"])
                        tt("dve", r0, r4(vtm[st][:, tl, :]), tmpf, ALU.subtract, [f"vtm{st}", "tmpf"], ["r0"])
                        tt("dve", o1, r4(p6), bc4(ecum[st][:, tl, :]), ALU.mult, [k6, f"ecum{st}"], ["o1"])
                        yield
                        (p8, k8) = nb("C")
                        (p9, k9) = nb("C")
                        for j in range(HG):
                            mm(p9[:, j * 128:(j + 1) * 128], M2Tl[st][i][:, j, :], r0[:, j, :], True, True, [f"M2T{st}{i}", "r0"], [k9])
                        for j in range(HG):
                            mm(p8[:, j * 128:(j + 1) * 128], Q2Tl[st][i][:, j, :], r0[:, j, :], True, True, [f"Q2T{st}{i}", "r0"], [k8])
                        yield
                        tt("dve", S32, S32, bc4(glast[st][:, tl, :]), ALU.mult, ["gS32", f"glast{st}"], ["gS32"])
                        tt("dve", S32, r4(p9), S32, ALU.add, [k9, "gS32"], ["gS32"])
                        cp("dve", Sbf, S32, ["gS32"], ["gSbf"])
                        tt("dve", o1, r4(p8), o1, ALU.add, [k8, "o1"], ["o1"])
                        yield
                        act(osq, o1, AF.Square, ["o1", "tmpf"], ["tmpf"])
                        S.op("dve", lambda e: e.tensor_reduce(out=sm[:, 0:4], in_=osq, axis=AX.X, op=ALU.add), ["tmpf"], ["ssq4"])
                        act(sm[:, 4:8], sm[:, 0:4], AF.Ln, ["ssq4"], ["sm48"], scale=1.0 / 128.0, bias=NORM_EPS)
                        act(sm[:, 4:8], sm[:, 4:8], AF.Exp, ["sm48"], ["sm48"], scale=-0.5)
                        tt("dve", o1, o1, bc4(sm[:, 4:8]), ALU.mult, ["o1", "sm48"], ["o1"])
                        tt("dve", ygd, o1, r4(zw[st][:, tl, :]), ALU.mult, ["o1", f"zw{st}"], ["ygd"])
                        yield
                        yield
                        yield
                        (pt, kt) = nb("C")
                        ptb = pt.bitcast(BF16)
                        for j in range(HG):
                            tr(ptb[:, j * 128:(j + 1) * 128], ygd[:, j, :], identb[:], ["ygd", "identb"], [kt])
                        cp("act", yTs[i % 2], ptb[:, 0:512].rearrange("p (a b) -> p a b", a=4), [kt], [f"gyTs{i % 2}"])
                        S.dma("sp", yTb[h0 * 128:(h0 + HG) * 128, t * 128:(t + 1) * 128].rearrange("(a p) n -> p a n", p=128),
                              yTs[i % 2], reads=[f"gyTs{i % 2}"], writes=[f"yTb{i % 2}"], grp=f"yTb{i % 2}")
                        yield
                    ev.add(("Cdone", u))

            run_streams([gdn_prep(), gdn_B(), gdn_C()])
            S.barrier()
            AR.reset()

        if stage >= 3:
            Y2 = AR.alloc([8192])
            yAB = Y2.bitcast(BF16).rearrange("p (a b c) -> p a b c", a=2, b=16)
            yA = yAB[:, 0]
            yB = yAB[:, 1]
            osb = Y2.rearrange("p (a b) -> p a b", a=4)
            mT = AR.alloc([16, 512], BF16)
            ga4 = AR.alloc([4, 512])
            gb4 = AR.alloc([4, 512])
            xt = AR.alloc([2048])
            gate_b = AR.alloc([2048])
            lng = AR.alloc([2048])
            lnb = AR.alloc([2048])
            stl = [AR.alloc([8]) for _ in range(4)]
            S.dma("sp", gate_b, gate_d[0, :].partition_broadcast(128), writes=["gate_b"], grp="gate_b")
            S.dma("sp", lng, ln_g.partition_broadcast(128), writes=["lng"], grp="lng")
            S.dma("sp", lnb, ln_b.partition_broadcast(128), writes=["lnb"], grp="lnb")
            for tb in range(4):
                S.dma("pool", yA, yTa[:, tb * 512:(tb + 1) * 512].rearrange("(kc p) n -> p kc n", p=128), writes=["Y2"], grp="yA")
                S.dma("pool", yB, yTb[:, tb * 512:(tb + 1) * 512].rearrange("(kc p) n -> p kc n", p=128), writes=["Y2"], grp="yB")
                for g4 in range(4):
                    bs, ks = load_slab([("in", C_MG + g4 * 512, 512)])
                    for q in range(4):
                        pv, pk = inproj_fm(bs, ks, q * 128, 128, tb)
                        act(ga4[:, q, :], pv, AF.Sigmoid, [pk], ["ga4"])
                    bs, ks = load_slab([("in", C_MG + 2048 + g4 * 512, 512)])
                    for q in range(4):
                        pv, pk = inproj_fm(bs, ks, q * 128, 128, tb)
                        act(gb4[:, q, :], pv, AF.Sigmoid, [pk], ["gb4"])
                    bs, ks = load_slab([("br0", g4 * 512, 512)])
                    for q in range(4):
                        (pa_, ka_) = nb()
                        for kc in range(16):
                            mm(pa_, slab[bs][:, kc, q * 128:(q + 1) * 128], yA[:, kc, :], kc == 0, kc == 15, ks + ["Y2"], [ka_])
                        tt("dve", ga4[:, q, :], pa_, ga4[:, q, :], ALU.mult, [ka_, "ga4"], ["ga4"])
                    bs, ks = load_slab([("br1", g4 * 512, 512)])
                    for q in range(4):
                        (pb_, kb_) = nb()
                        for kc in range(16):
                            mm(pb_, slab[bs][:, kc, q * 128:(q + 1) * 128], yB[:, kc, :], kc == 0, kc == 15, ks + ["Y2"], [kb_])
                        tt("dve", gb4[:, q, :], pb_, gb4[:, q, :], ALU.mult, [kb_, "gb4"], ["gb4"])
                        tt("dve", mT[:, g4 * 4 + q, :], ga4[:, q, :], gb4[:, q, :], ALU.add, ["ga4", "gb4"], ["mT"])
                for nbk in range(4):
                    bo, ko = load_slab([("out", nbk * 512, 512)])
                    for tq in range(4):
                        (po, kk) = nb()
                        for kc in range(16):
                            mm(po, mT[:, kc, tq * 128:(tq + 1) * 128], slab[bo][:, kc, :], kc == 0, kc == 15, ko + ["mT"], [kk])
                        tt("dve", osb[:, tq, nbk * 512:(nbk + 1) * 512], po, gate_b[:, nbk * 512:(nbk + 1) * 512], ALU.mult,
                           [kk, "gate_b", "Y2"], [("osb", tq)])
                mTf = mT.rearrange("p a b -> p (a b)").bitcast(F32)

                def ln_tile(tq, xb, xk, dump, dk):
                    t = tb * 4 + tq
                    OK = ("osb", tq)
                    S.dma("act", xb, x[t * 128:(t + 1) * 128, :], writes=[xk], grp="x" + xk)
                    r = osb[:, tq, :]
                    yield
                    stt("dve", r, xb, ALPHA, r, ALU.mult, ALU.add, [xk, "Y2", OK], [OK])
                    yield
                    act(dump, r, AF.Identity, [OK], [dk, ("st0", tq)], accum_out=stl[tq][:, 0:1])
                    yield
                    ts("dve", stl[tq][:, 1:2], stl[tq][:, 0:1], -1.0 / D, None, ALU.mult, None, [("st0", tq)], [("st1", tq)])
                    yield
                    act(r, r, AF.Identity, [OK, ("st1", tq)], [OK], bias=stl[tq][:, 1:2])
                    act(dump, r, AF.Square, [OK], [dk, ("st2", tq)], accum_out=stl[tq][:, 2:3])
                    act(stl[tq][:, 3:4], stl[tq][:, 2:3], AF.Ln, [("st2", tq)], [("st3", tq)], scale=1.0 / D, bias=LN_EPS)
                    act(stl[tq][:, 4:5], stl[tq][:, 3:4], AF.Exp, [("st3", tq)], [("st4", tq)], scale=-0.5)
                    yield
                    stt("dve", r, r, stl[tq][:, 4:5], lng, ALU.mult, ALU.mult, [OK, ("st4", tq), "lng"], [OK])
                    yield
                    tt("dve", r, r, lnb, ALU.add, [OK, "lnb"], [OK])
                    yield
                    S.dma("act", out[t * 128:(t + 1) * 128, :], r, reads=[OK], writes=[("OUT", tq)], grp=f"out{tq}", guard=["Y2"])

                ga4f = ga4.rearrange("p a b -> p (a b)")
                gb4f = gb4.rearrange("p a b -> p (a b)")
                run_streams([ln_tile(0, xt, "xt", xt, "xt"), ln_tile(1, mTf[:, 0:2048], "mT", mTf[:, 2048:4096], "mTd")])
                run_streams([ln_tile(2, xt, "xt", xt, "xt"), ln_tile(3, mTf[:, 0:2048], "mT", mTf[:, 2048:4096], "mTd")])
                S.op("pool", lambda e: e.nop(), ["mTd"], ["mT"])
                S.op("pool", lambda e: e.nop(), [("osb", q_) for q_ in range(4)] + [("OUT", q_) for q_ in range(4)], ["Y2"])
        S.barrier()
        S.emit(final_wait_keys=[])
        build.stats = dict(nops=len(S.ops), nsem=S.nsem, maxval=S.maxval, arena=AR.off)
    return nc


_NC_CACHE = {}


def kernel(**inputs):
    stage = 9
    if stage not in _NC_CACHE:
        _NC_CACHE[stage] = build(stage)
    nc = _NC_CACHE[stage]
    cst = host_consts()
    f = lambda a: np.ascontiguousarray(np.asarray(a, dtype=np.float32))
    shared = dict(
        w_ada=f(inputs["w_ada"][0]), b_ada=f(inputs["b_ada"][0]), w_in=f(inputs["w_in"][0]),
        w_gk2=f(inputs["w_gk2"][0]), b_gk2=f(inputs["b_gk2"][0]), conv_w=f(inputs["conv_w"][0]),
        a_log=f(inputs["a_log"][0]), dt_bias=f(inputs["dt_bias"][0]),
        gla_norm_w=f(inputs["gla_norm_w"][0]), gdn_norm_w=f(inputs["gdn_norm_w"][0]),
        w_branch=f(inputs["w_branch"][0]), w_out=f(inputs["w_out"][0]),
        ln_g=f(inputs["ln_g"][0]), ln_b=f(inputs["ln_b"][0]), cst=cst)
    x = np.asarray(inputs["x"], dtype=np.float32)
    c = np.asarray(inputs["c"], dtype=np.float32)
    in_maps = []
    for b in range(8):
        m = dict(shared)
        m["x"] = np.ascontiguousarray(x[b])
        m["c"] = np.ascontiguousarray(c[b])
        in_maps.append(m)
    res = run_bass_kernel_spmd(nc, in_maps, core_ids=list(range(8)))
    return np.stack([np.asarray(r["out"], dtype=np.float32) for r in res.results], axis=0)
```

```python
import numpy as np
import concourse.bass as bass
import concourse.mybir as mybir
from concourse.bass_utils import run_bass_kernel_spmd
from contextlib import ExitStack

F32 = mybir.dt.float32
BF16 = mybir.dt.bfloat16
AF = mybir.ActivationFunctionType
ALU = mybir.AluOpType
AX = mybir.AxisListType

ENGS = ("pe", "dve", "act", "pool", "sp")

D = 2048
SEQ = 2048
NT = 16
IN_TOTAL = 18480
C_GLQ, C_GLK, C_GLV, C_GLG, C_LR = 0, 1024, 2048, 4096, 6144
C_GDQ, C_GDK, C_GDV, C_GDZ, C_GDB, C_GDA, C_MG = 6160, 8208, 10256, 12304, 14352, 14368, 14384
ALPHA = 2.0 ** 0.25
NORM_EPS = 1e-6
LN_EPS = 1e-5

K_ID, K_TRIU, K_MS, K_ONES, K_NEG, K_NEGMT = 0, 128, 256, 384, 512, 1408
NCF = 1536


class Sched:
    def __init__(self, nc, es):
        self.nc = nc
        self.es = es
        self.q = {e: [] for e in ENGS}
        self.ops = []
        self.lastw = {}
        self.readers = {}
        self.seen = {e: {} for e in ENGS}
        self.chan_ops = {}

    def _add(self, eng, fn, reads, writes, chan, extra_deps=(), guard=()):
        oid = len(self.ops)
        deps = set(extra_deps)
        for k in guard:
            w = self.lastw.get(k)
            if w is not None:
                deps.add(w)
            deps.update(self.readers.get(k, ()))
        for k in reads:
            w = self.lastw.get(k)
            if w is not None:
                deps.add(w)
        for k in writes:
            w = self.lastw.get(k)
            if w is not None:
                deps.add(w)
            for r in self.readers.get(k, ()):
                o = self.ops[r]
                if o["chan"] == eng and chan == eng:
                    continue
                deps.add(r)
        waits = {}
        for d in deps:
            o = self.ops[d]
            c = o["chan"]
            if c == eng and eng == "pe":
                continue
            if self.seen[eng].get(c, -1) >= o["pos"]:
                continue
            waits[c] = max(waits.get(c, -1), o["pos"])
        for c, p in waits.items():
            self.seen[eng][c] = p
            self.ops[self.chan_ops[c][p]]["signal"] = True
        lst = self.chan_ops.setdefault(chan, [])
        op = dict(id=oid, eng=eng, fn=fn, chan=chan, pos=len(lst), waits=waits,
                  signal=isinstance(chan, tuple))
        lst.append(oid)
        self.ops.append(op)
        self.q[eng].append(op)
        for k in reads:
            self.readers.setdefault(k, []).append(oid)
        for k in writes:
            self.lastw[k] = oid
            self.readers[k] = []
        return oid

    def op(self, eng, fn, reads=(), writes=()):
        return self._add(eng, fn, tuple(reads), tuple(writes), eng)

    def dma(self, eng, out, in_, reads=(), writes=(), grp=None, guard=(), **kw):
        fn = lambda e, out=out, in_=in_, kw=kw: e.dma_start(out=out, in_=in_, **kw)
        return self._add(eng, fn, tuple(reads), tuple(writes), ("dma", grp), guard=tuple(guard))

    def barrier(self):
        def pers(c):
            return isinstance(c, tuple) and str(c[1]).startswith("cv")
        last = [lst[-1] for c, lst in self.chan_ops.items() if lst and not pers(c)]
        for e in ENGS:
            self._add(e, lambda en: en.nop(), (), (), e, extra_deps=last)
        self.lastw = {k: v for k, v in self.lastw.items() if isinstance(k, str) and k.startswith("cv")}
        self.readers = {}

    def emit(self, final_wait_keys=()):
        nc = self.nc
        fw = {}
        for k in final_wait_keys:
            w = self.lastw.get(k)
            if w is not None:
                o = self.ops[w]
                fw[o["chan"]] = max(fw.get(o["chan"], -1), o["pos"])
                o["signal"] = True
        sems = {}
        vals = {}
        for c, lst in self.chan_ops.items():
            n = 0
            step = 16 if isinstance(c, tuple) else 1
            anysig = False
            for oid in lst:
                o = self.ops[oid]
                if o["signal"]:
                    n += step
                    anysig = True
                vals[oid] = n
            if anysig:
                nm = "s_" + (c if isinstance(c, str) else "d_" + str(c[1]))
                sems[c] = self.es.enter_context(nc.semaphore(nm))
        self.nsem = len(sems)
        self.maxval = max(vals.values()) if vals else 0

        def run(ename, e):
            for o in self.q[ename]:
                for c, p in o["waits"].items():
                    e.wait_ge(sems[c], vals[self.chan_ops[c][p]])
                ins = o["fn"](e)
                if o["signal"]:
                    ins.then_inc(sems[o["chan"]], 16 if isinstance(o["chan"], tuple) else 1)
            if ename == "sp":
                for c, p in fw.items():
                    e.wait_ge(sems[c], vals[self.chan_ops[c][p]])

        with nc.Block() as block:
            @block.tensor
            def _(e):
                run("pe", e)

            @block.vector
            def _(e):
                run("dve", e)

            @block.scalar
            def _(e):
                run("act", e)

            @block.gpsimd
            def _(e):
                run("pool", e)

            @block.sync
            def _(e):
                run("sp", e)


class Arena:
    def __init__(self, ap):
        self.ap = ap
        self.n = ap.shape[1]
        self.off = 0

    def alloc(self, free, dt=F32, parts=128):
        n = 1
        for f in free:
            n *= f
        words = n if dt == F32 else (n + 1) // 2
        assert self.off + words <= self.n, f"arena overflow {self.off}+{words}>{self.n}"
        v = self.ap[:, self.off:self.off + words]
        self.off += words
        if dt == BF16:
            v = v.bitcast(BF16)
            if n % 2:
                v = v[:, 0:n]
        if len(free) == 2:
            v = v.rearrange("p (a b) -> p a b", a=free[0])
        elif len(free) == 3:
            v = v.rearrange("p (a b c) -> p a b c", a=free[0], b=free[1])
        if parts < 128:
            v = v[0:parts]
        return v

    def reset(self):
        self.off = 0


def run_streams(gens):
    active = list(gens)
    while active:
        for g in list(active):
            try:
                next(g)
            except StopIteration:
                active.remove(g)


def host_consts():
    r = np.arange(128)
    c = np.zeros((128, NCF), np.float32)
    c[:, K_ID:K_ID + 128] = np.eye(128)
    c[:, K_TRIU:K_TRIU + 128] = (r[:, None] <= r[None, :])
    c[:, K_MS:K_MS + 128] = (r[:, None] > r[None, :])
    c[:, K_ONES:K_ONES + 128] = 1.0
    for j in range(7):
        b = 1 << j
        t = r[:, None]
        s = r[None, :]
        m = ((t // (2 * b)) == (s // (2 * b))) & ((t % (2 * b)) >= b) & ((s % (2 * b)) < b)
        c[:, K_NEG + j * 128:K_NEG + (j + 1) * 128] = -m.astype(np.float32)
    c[:, K_NEGMT:K_NEGMT + 128] = -100.0 * (r[None, :] <= r[:, None])
    return c


def build(stage=9, debug=False):
    nc = bass.Bass("TRN2", target_bir_lowering=False)

    def din(name, shape, dt=F32):
        return nc.dram_tensor(name, list(shape), dt, kind="ExternalInput").ap()

    x = din("x", [SEQ, D])
    cvec = din("c", [D])
    w_ada = din("w_ada", [D, 3 * D])
    b_ada = din("b_ada", [3 * D])
    w_in = din("w_in", [D, IN_TOTAL])
    w_gk2 = din("w_gk2", [16, 1024])
    b_gk2 = din("b_gk2", [1024])
    conv_w = din("conv_w", [4, 6144])
    a_log = din("a_log", [16])
    dt_bias = din("dt_bias", [16])
    gla_nw = din("gla_norm_w", [512])
    gdn_nw = din("gdn_norm_w", [128])
    w_br = din("w_branch", [2, D, D])
    w_out = din("w_out", [D, D])
    ln_g = din("ln_g", [D])
    ln_b = din("ln_b", [D])
    cst_d = din("cst", [128, NCF])
    out = nc.dram_tensor("out", [SEQ, D], F32, kind="ExternalOutput").ap()
    skind = "ExternalOutput" if debug else "Internal"
    yTa = nc.dram_tensor("yTa", [D, SEQ], BF16, kind=skind).ap()
    yTb = nc.dram_tensor("yTb", [D, SEQ], BF16, kind=skind).ap()
    gate_d = nc.dram_tensor("gate_d", [1, D], F32, kind="Internal").ap()
    wsrc = {"ada": w_ada, "in": w_in, "br0": w_br[0], "br1": w_br[1], "out": w_out}
    wbf = {k: nc.dram_tensor("wbf_" + k, list(v.shape), BF16, kind="Internal").ap() for k, v in wsrc.items()}
    CVB = 2048
    dbg = nc.dram_tensor("dbg", [8, 128, 512], F32, kind="ExternalOutput").ap() if debug else None

    es = ExitStack()
    with es:
        S = Sched(nc, es)

        def sbt(name, shape, dt=F32):
            return es.enter_context(nc.sbuf_tensor(name, shape, dt))

        hT = sbt("hT", [128, 16, SEQ], BF16)
        slab = [sbt(f"slab{i}", [128, 16, 512], BF16) for i in range(2)]
        cst = sbt("cstf", [128, NCF], F32)
        identb = sbt("identb", [128, 128], BF16)
        onesb = sbt("onesb", [128, 128], BF16)
        negmt4 = sbt("negmt4", [128, 4, 128], BF16)
        i4 = sbt("i4", [128, 4, 128], BF16)
        adaT = sbt("adaT", [128, 32], F32)
        ARW = 26200
        arena_t = sbt("arena", [128, ARW], F32)
        AR = Arena(arena_t[:])
        PA = es.enter_context(nc.psum_tensor("PA", [128, 2048], F32))
        PB = es.enter_context(nc.psum_tensor("PB", [128, 2048], F32))
        banks = [(PA[:, i * 512:(i + 1) * 512], f"PA{i}") for i in range(4)] + \
                [(PB[:, i * 512:(i + 1) * 512], f"PB{i}") for i in range(4)]
        bank_i = [0]

        bank_pools = {"P": [0, 1], "B": [2, 3, 4, 5], "C": [6, 7], "G": [2, 3, 4, 5, 6, 7]}
        pool_i = {"P": 0, "B": 0, "C": 0, "G": 0}

        def nb(pool=None):
            if pool is None:
                b = banks[bank_i[0] % 8]
                bank_i[0] += 1
                return b
            lst = bank_pools[pool]
            b = banks[lst[pool_i[pool] % len(lst)]]
            pool_i[pool] += 1
            return b

        ident = cst[:, K_ID:K_ID + 128]
        triU = cst[:, K_TRIU:K_TRIU + 128]
        maskS = cst[:, K_MS:K_MS + 128]
        onesf = cst[:, K_ONES:K_ONES + 128]

        def mm(o, lhsT, rhs, start, stop, reads, writes):
            S.op("pe", lambda e: e.matmul(o, lhsT=lhsT, rhs=rhs, start=start, stop=stop), reads, writes)

        def tr(o, in_, idn, reads, writes):
            S.op("pe", lambda e: e.transpose(out=o, in_=in_, identity=idn), reads, writes)

        def act(o, in_, func, reads, writes, **kw):
            S.op("act", lambda e: e.activation(out=o, in_=in_, func=func, **kw), reads, writes)

        def tt(eng, o, in0, in1, op, reads, writes):
            S.op(eng, lambda e: e.tensor_tensor(out=o, in0=in0, in1=in1, op=op), reads, writes)

        def ts(eng, o, in0, s1, s2, op0, op1, reads, writes):
            if s2 is None:
                S.op(eng, lambda e: e.tensor_scalar(out=o, in0=in0, scalar1=s1, scalar2=None, op0=op0), reads, writes)
            else:
                S.op(eng, lambda e: e.tensor_scalar(out=o, in0=in0, scalar1=s1, scalar2=s2, op0=op0, op1=op1), reads, writes)

        def stt(eng, o, in0, sc, in1, op0, op1, reads, writes):
            S.op(eng, lambda e: e.scalar_tensor_tensor(out=o, in0=in0, scalar=sc, in1=in1, op0=op0, op1=op1), reads, writes)

        def cp(eng, o, in_, reads, writes):
            if eng == "act":
                S.op("act", lambda e: e.copy(out=o, in_=in_), reads, writes)
            else:
                S.op(eng, lambda e: e.tensor_copy(out=o, in_=in_), reads, writes)

        def memset(eng, o, val, writes):
            S.op(eng, lambda e: e.memset(o, val), (), writes)

        slab_n = [0]

        CVBN = {"ada": 2048, "in": 512, "br0": 2048, "br1": 2048, "out": 2048}
        cv_pending = []
        n_in = (IN_TOTAL + 511) // 512
        for blk_ in range(12, n_in):
            cv_pending.append(("in", blk_ * 512))
        for name_ in ("br0", "br1", "out"):
            for c0_ in range(0, wsrc[name_].shape[1], CVBN[name_]):
                cv_pending.append((name_, c0_))

        def convert_more(k):
            for _ in range(k):
                if not cv_pending:
                    return
                name, c0 = cv_pending.pop(0)
                n = min(CVBN[name], wsrc[name].shape[1] - c0)
                kk = f"cv_{name}_{c0 // CVBN[name]}"
                S.dma("pool", wbf[name][:, c0:c0 + n], wsrc[name][:, c0:c0 + n], writes=[kk], grp=kk)

        def load_slab(pieces, direct=False):
            b = slab_n[0] % 2
            slab_n[0] += 1
            off = 0
            keys = []
            for pi, (name, c0, n) in enumerate(pieces):
                k = f"slab{b}.{pi}"
                if direct:
                    S.dma("pool", slab[b][:, :, off:off + n],
                          wsrc[name][:, c0:c0 + n].rearrange("(kc p) n -> p kc n", p=128),
                          writes=[k], grp=k, guard=[f"slab{b}.{i}" for i in range(4)])
                    keys.append(k)
                    off += n
                    continue
                cvk = [f"cv_{name}_{i}" for i in range(c0 // CVBN[name], (c0 + n - 1) // CVBN[name] + 1)]
                S.dma("sp", slab[b][:, :, off:off + n],
                      wbf[name][:, c0:c0 + n].rearrange("(kc p) n -> p kc n", p=128),
                      reads=cvk, writes=[k], grp=k, guard=[f"slab{b}.{i}" for i in range(4)])
                keys.append(k)
                off += n
            return b, keys

        def slab_keys_all(b):
            return [f"slab{b}.{i}" for i in range(2)]

        def inproj_fm(b, skeys, coff, m, tb, width=512):
            (pv, pk) = nb()
            for kc in range(16):
                mm(pv[0:m, 0:width], slab[b][:, kc, coff:coff + m], hT[:, kc, tb * width:(tb + 1) * width],
                   kc == 0, kc == 15, list(skeys) + ["hT"], [pk])
            return pv, pk

        def inproj_tm(b, skeys, coff, n, tile, pv=None, pk=None, pcol=0):
            if pv is None:
                (pv, pk) = nb()
            for kc in range(16):
                mm(pv[:, pcol:pcol + n], hT[:, kc, tile * 128:(tile + 1) * 128], slab[b][:, kc, coff:coff + n],
                   kc == 0, kc == 15, list(skeys) + ["hT"], [pk])
            return pv, pk

        S.dma("sp", cst[:], cst_d, writes=["cst"], grp="cst")
        cp("dve", identb[:], ident, ["cst"], ["identb"])
        cp("dve", onesb[:], onesf, ["cst"], ["onesb"])
        cp("dve", negmt4[:], cst[:, K_NEGMT:K_NEGMT + 128].unsqueeze(1).broadcast_to([128, 4, 128]), ["cst"], ["negmt4"])
        cp("dve", i4[:], ident.unsqueeze(1).broadcast_to([128, 4, 128]), ["cst"], ["i4"])

        cs = AR.alloc([16])
        scs = AR.alloc([16])
        sc16 = AR.alloc([16], BF16)
        scB = AR.alloc([16, 128], BF16)
        badT = AR.alloc([32])
        gate_bc = AR.alloc([2048])
        cs16 = AR.alloc([128], F32, parts=16)
        b32 = AR.alloc([128], F32, parts=32)
        S.dma("sp", cs16, cvec.rearrange("(kc p) -> kc p", p=128), writes=["cs16"], grp="cs16")
        S.dma("sp", b32, b_ada[0:4096].rearrange("(c p) -> c p", p=128), writes=["b32"], grp="b32")
        S.dma("sp", gate_bc, b_ada[4096:6144].partition_broadcast(128), writes=["gate_bc"], grp="gate_bc")
        (px0, kx0) = nb()
        tr(px0[:, 0:16], cs16, cst[0:16, K_ID:K_ID + 16], ["cs16", "cst"], [kx0])
        tr(px0[:, 16:48], b32, cst[0:32, K_ID:K_ID + 32], ["b32", "cst"], [kx0])
        cp("dve", cs, px0[:, 0:16], [kx0], ["cs"])
        cp("dve", badT, px0[:, 16:48], [kx0], ["badT"])
        act(scs, cs, AF.Silu, ["cs"], ["scs"])
        cp("dve", sc16, scs, ["scs"], ["sc16"])
        cp("dve", scB, scs.unsqueeze(2).broadcast_to([128, 16, 128]), ["scs"], ["scB"])
        (pada, padak) = nb()

        def ada_slab(sl):
            b, keys = load_slab([("ada", sl * 512, 512)], direct=True)
            if sl < 8:
                for q in range(4):
                    col = sl * 4 + q
                    for kc in range(16):
                        mm(pada[:, col:col + 1], slab[b][:, kc, q * 128:(q + 1) * 128], sc16[:, kc:kc + 1],
                           kc == 0, kc == 15, keys + ["sc16"], [padak])
            else:
                (pv, pk) = nb()
                for kc in range(16):
                    mm(pv, scB[:, kc, :], slab[b][:, kc, :], kc == 0, kc == 15, keys + ["scB"], [pk])
                blk = gate_bc[:, (sl - 8) * 512:(sl - 7) * 512]
                tt("dve", blk, pv, blk, ALU.add, [pk, "gate_bc"], ["gate_bc"])

        xs = AR.alloc([4, 2048])
        S.dma("sp", xs, x[0:512, :].rearrange("(j p) d -> p j d", p=128), writes=["xs"], grp="xs")
        for sl in range(8):
            ada_slab(sl)
        tt("dve", adaT[:], pada[:, 0:32], badT, ALU.add, [padak, "badT"], ["adaT"])
        ts("dve", adaT[:, 16:32], adaT[:, 16:32], 1.0, None, ALU.add, None, ["adaT"], ["adaT"])
        for tb in range(4):
            if tb > 0:
                S.dma("sp", xs, x[tb * 512:(tb + 1) * 512, :].rearrange("(j p) d -> p j d", p=128), writes=["xs"], grp="xs")
            for dc in range(16):
                (pv, pk) = nb()
                for j in range(4):
                    tr(pv[:, j * 128:(j + 1) * 128], xs[:, j, dc * 128:(dc + 1) * 128], ident, ["xs", "cst"], [pk])
                act(hT[:, dc, tb * 512:(tb + 1) * 512], pv, AF.Identity, [pk, "adaT"], [("hT", tb, dc)],
                    bias=adaT[:, dc:dc + 1], scale=adaT[:, 16 + dc:17 + dc])
            ada_slab(8 + tb)
        S.dma("sp", gate_d, gate_bc[0:1, :], reads=["gate_bc"], writes=["gate_d"], grp="gate_d")
        convert_more(8)
        S.barrier()
        AR.reset()

        if stage >= 1:
            lrT = AR.alloc([2048], F32, parts=32)
            wg2 = AR.alloc([1024], F32, parts=32)
            glw = AR.alloc([512])
            qT = [AR.alloc([2, 1024], BF16) for _ in range(2)]
            kT = [AR.alloc([2, 1024], BF16) for _ in range(2)]
            vt = [AR.alloc([8, 512], BF16) for _ in range(2)]
            gw = [AR.alloc([8, 512], BF16) for _ in range(2)]
            yTs = AR.alloc([4, 512], BF16)
            S32 = AR.alloc([2, 512])
            Sbf = AR.alloc([2, 512], BF16)
            pp = [AR.alloc([256]) for _ in range(2)]
            eqk = [AR.alloc([2, 256]) for _ in range(2)]
            qtl = [AR.alloc([2, 128], BF16) for _ in range(2)]
            ktl = [AR.alloc([2, 128], BF16) for _ in range(2)]
            ktm = [AR.alloc([256], BF16) for _ in range(2)]
            attT = [AR.alloc([128], BF16) for _ in range(2)]
            dS = [AR.alloc([2, 512]) for _ in range(2)]
            ytile = AR.alloc([512], BF16)
            gsil = AR.alloc([512])
            small = AR.alloc([8])
            junk = AR.alloc([512])
            triUb = AR.alloc([128], BF16)
            cp("dve", triUb, triU, [], ["triUb"])
            memset("dve", lrT, 1.0, ["lrT"])
            memset("dve", wg2, 0.0, ["wg2"])
            S.dma("sp", wg2[0:16, :], w_gk2, reads=[], writes=["wg2"], grp="wg2a")
            S.dma("sp", wg2[16:17, :], b_gk2.rearrange("(o n) -> o n", o=1), reads=[], writes=["wg2"], grp="wg2b")
            S.dma("sp", glw, gla_nw.partition_broadcast(128), writes=["glw"], grp="glw")
            b, keys = load_slab([("in", C_LR, 16)], direct=True)
            for tb in range(4):
                pv, pk = inproj_fm(b, keys, 0, 16, tb)
                cp("dve", lrT[0:16, tb * 512:(tb + 1) * 512], pv[0:16, :], [pk, "lrT"], ["lrT"])
            def gla_pre(hd, st, half, tl):
                t = half * 8 + tl
                lsl = slice(tl * 128, (tl + 1) * 128)
                s2 = tl % 2
                Z = str(s2)
                tsl = slice(t * 128, (t + 1) * 128)
                (p1, k1) = nb("G")
                mm(p1[:, 0:256], lrT[0:32, tsl], wg2[0:32, hd * 256:(hd + 1) * 256], True, True, ["lrT", "wg2"], [k1])
                yield
                act(pp[s2], p1[:, 0:256], AF.Exp, [k1], ["pp" + Z], scale=-1.0)
                act(pp[s2], pp[s2], AF.Ln, ["pp" + Z], ["pp" + Z], bias=1.0)
                yield
                (p2, k2) = nb("G")
                for c2 in range(2):
                    mm(p2[:, c2 * 128:(c2 + 1) * 128], pp[s2][:, c2 * 128:(c2 + 1) * 128], triU, True, True, ["pp" + Z, "cst"], [k2])
                yield
                act(eqk[s2][:, 0, :], p2[:, 0:256], AF.Exp, [k2], ["eq" + Z], scale=-1.0 / 16.0)
                act(eqk[s2][:, 1, :], p2[:, 0:256], AF.Exp, [k2], ["ek" + Z], scale=1.0 / 16.0)
                yield
                for c2 in range(2):
                    stt("dve", qtl[s2][:, c2, :], qT[st][:, c2, lsl], 256.0 ** -0.5, eqk[s2][:, 0, c2 * 128:(c2 + 1) * 128],
                        ALU.mult, ALU.mult, [f"gqT{st}", "eq" + Z], ["qtl" + Z])
                    tt("dve", ktl[s2][:, c2, :], kT[st][:, c2, lsl], eqk[s2][:, 1, c2 * 128:(c2 + 1) * 128], ALU.mult,
                       [f"gkT{st}", "ek" + Z], ["ktl" + Z])
                yield
                (p3, k3) = nb("G")
                p3b = p3.bitcast(BF16)
                for c2 in range(2):
                    tr(p3b[:, c2 * 128:(c2 + 1) * 128], ktl[s2][:, c2, :], identb[:], ["ktl" + Z, "identb"], [k3])
                yield
                cp("dve", ktm[s2], p3b[:, 0:256], [k3], ["ktm" + Z])
                (p4, k4) = nb("G")
                for c2 in range(2):
                    mm(p4[:, 0:128], ktl[s2][:, c2, :], qtl[s2][:, c2, :], c2 == 0, c2 == 1, ["ktl" + Z, "qtl" + Z], [k4])
                yield
                tt("dve", attT[s2], p4[:, 0:128], triU, ALU.mult, [k4, "cst"], ["attT" + Z])
                p7s = []
                for c2 in range(2):
                    (p7, k7) = nb("G")
                    mm(p7, ktm[s2][:, c2 * 128:(c2 + 1) * 128], vt[st][:, tl, :], True, True, ["ktm" + Z, f"gvt{st}"], [k7])
                    p7s.append((p7, k7))
                yield
                for c2 in range(2):
                    (p7, k7) = p7s[c2]
                    el = eqk[s2][:, 0, c2 * 128 + 127:c2 * 128 + 128]
                    act(dS[s2][:, c2, :], p7, AF.Copy, [k7, "eq" + Z], ["dS" + Z], scale=el)

            def gla_rec(hd, st, half, tl):
                t = half * 8 + tl
                s2 = tl % 2
                Z = str(s2)
                (p5, k5) = nb("G")
                for c2 in range(2):
                    mm(p5, qtl[s2][:, c2, :], Sbf[:, c2, :], c2 == 0, False, ["qtl" + Z, "Sbf"], [k5])
                mm(p5, attT[s2], vt[st][:, tl, :], False, True, ["attT" + Z, f"gvt{st}"], [k5])
                yield
                for c2 in range(2):
                    el = eqk[s2][:, 0, c2 * 128 + 127:c2 * 128 + 128]
                    stt("dve", S32[:, c2, :], S32[:, c2, :], el, dS[s2][:, c2, :], ALU.mult, ALU.add, ["S32", "eq" + Z, "dS" + Z], ["S32"])
                    cp("dve", Sbf[:, c2, :], S32[:, c2, :], ["S32"], ["Sbf"])
                yield
                act(junk, p5, AF.Square, [k5], ["junk", "ssq"], accum_out=small[:, 0:1])
                act(small[:, 1:2], small[:, 0:1], AF.Ln, ["ssq"], ["lssq"], scale=1.0 / 512.0, bias=NORM_EPS)
                act(small[:, 2:3], small[:, 1:2], AF.Exp, ["lssq"], ["rstd"], scale=-0.5)
                yield
                stt("dve", ytile, p5, small[:, 2:3], gw[st][:, tl, :], ALU.mult, ALU.mult, [k5, "rstd", f"ggw{st}"], ["ytile"])
                yield
                (p6, k6) = nb("G")
                p6b = p6.bitcast(BF16)
                for c4 in range(4):
                    tr(p6b[:, c4 * 128:(c4 + 1) * 128], ytile[:, c4 * 128:(c4 + 1) * 128], identb[:], ["ytile", "identb"], [k6])
                yield
                cp("act", yTs[:, :, (t % 4) * 128:(t % 4 + 1) * 128],
                   p6b[:, 0:512].rearrange("p (a b) -> p a b", a=4), [k6], ["yTs"])
                if t % 4 == 3:
                    S.dma("sp", yTa[hd * 512:(hd + 1) * 512, (t // 4) * 512:(t // 4 + 1) * 512].rearrange("(a p) n -> p a n", p=128),
                          yTs, reads=["yTs"], writes=["yTa"], grp="yTa")


            gev = set()

            def gla_inproj():
                for u in range(8):
                    hd, half = divmod(u, 2)
                    st = u % 2
                    while u >= 2 and ("Cdone", u - 2) not in gev:
                        yield
                    b, keys = load_slab([("in", C_GLQ + hd * 256, 256), ("in", C_GLK + hd * 256, 256)], direct=True)
                    for tb2 in range(2):
                        for (dst, co, dk_) in ((qT, 0, "gqT"), (kT, 256, "gkT")):
                            for c2 in range(2):
                                (pv, pk) = nb("P")
                                for kc in range(16):
                                    mm(pv, slab[b][:, kc, co + c2 * 128:co + (c2 + 1) * 128],
                                       hT[:, kc, (half * 2 + tb2) * 512:(half * 2 + tb2 + 1) * 512],
                                       kc == 0, kc == 15, list(keys) + ["hT"], [pk])
                                    if kc % 4 == 3:
                                        yield
                                cp("act", dst[st][:, c2, tb2 * 512:(tb2 + 1) * 512], pv, [pk], [f"{dk_}{st}"])
                    b, keys = load_slab([("in", C_GLV + hd * 512, 512)], direct=True)
                    for tl in range(8):
                        (pv, pk) = nb("P")
                        for kc in range(16):
                            mm(pv, hT[:, kc, (half * 8 + tl) * 128:(half * 8 + tl + 1) * 128], slab[b][:, kc, 0:512],
                               kc == 0, kc == 15, list(keys) + ["hT"], [pk])
                            if kc % 4 == 3:
                                yield
                        cp("act", vt[st][:, tl, :], pv, [pk], [f"gvt{st}"])
                    b, keys = load_slab([("in", C_GLG + hd * 512, 512)], direct=True)
                    for tl in range(8):
                        (pv, pk) = nb("P")
                        for kc in range(16):
                            mm(pv, hT[:, kc, (half * 8 + tl) * 128:(half * 8 + tl + 1) * 128], slab[b][:, kc, 0:512],
                               kc == 0, kc == 15, list(keys) + ["hT"], [pk])
                            if kc % 4 == 3:
                                yield
                        act(gsil, pv, AF.Silu, [pk], ["gsil"])
                        tt("dve", gw[st][:, tl, :], gsil, glw, ALU.mult, ["gsil", "glw"], [f"ggw{st}"])
                    gev.add(("ready", u))

            def gla_consumer():
                for u in range(8):
                    hd, half = divmod(u, 2)
                    st = u % 2
                    while ("ready", u) not in gev:
                        yield
                    if half == 0:
                        memset("dve", S32, 0.0, ["S32"])
                        memset("dve", Sbf, 0.0, ["Sbf"])
                    for _ in gla_pre(hd, st, half, 0):
                        yield
                    for tl in range(8):
                        if tl % 2 == 1:
                            convert_more(2)
                        subs = ([gla_pre(hd, st, half, tl + 1)] if tl + 1 < 8 else []) + [gla_rec(hd, st, half, tl)]
                        while subs:
                            for g in list(subs):
                                try:
                                    next(g)
                                except StopIteration:
                                    subs.remove(g)
                            yield
                    gev.add(("Cdone", u))

            run_streams([gla_inproj(), gla_consumer()])
            convert_more(1000)
            S.barrier()
            AR.reset()

        if stage >= 2:
            HG = 4
            TH = 8
            HS = TH * 128
            cacc = AR.alloc([512])
            lt = AR.alloc([512])
            ba = cacc.rearrange("p (a b) -> p a b", a=16)
            betat = AR.alloc([16, 16])
            gt = AR.alloc([16, 16])
            negA = AR.alloc([16])
            dtb = AR.alloc([16])
            gdw = AR.alloc([128])
            cw = AR.alloc([192])
            cwr = lt[0:96, 0:256].rearrange("p (a b) -> p a b", a=2)
            S.dma("sp", negA, a_log.partition_broadcast(128), writes=["negA"], grp="negA")
            S.dma("sp", dtb, dt_bias.partition_broadcast(128), writes=["dtb"], grp="dtb")
            S.dma("sp", gdw, gdn_nw.partition_broadcast(128), writes=["gdw"], grp="gdw")
            cw2 = conv_w.rearrange("j (c p) -> (j c) p", p=128)
            (pxc, kxc) = nb()
            for i2 in range(2):
                S.dma("sp", cwr[:, i2, :], cw2[i2 * 96:(i2 + 1) * 96, :], writes=["lt"], grp=f"cwr{i2}")
                tr(pxc[:, i2 * 96:(i2 + 1) * 96], cwr[:, i2, :], cst[0:96, K_ID:K_ID + 96], ["lt", "cst"], [kxc])
            cp("dve", cw, pxc[:, 0:192], [kxc], ["cw"])
            act(negA, negA, AF.Exp, ["negA"], ["negA"])
            ts("dve", negA, negA, -1.0, None, ALU.mult, None, ["negA"], ["negA"])
            b, keys = load_slab([("in", C_GDB, 32)])
            (pv, pk) = nb()
            for t in range(NT):
                inproj_tm(b, keys, 0, 32, t, pv, pk, pcol=t * 32)
            cp("dve", ba, pv.rearrange("p (a b) -> p a b", a=16), [pk], ["cacc"])
            act(betat, ba[:, :, 0:16], AF.Exp, ["cacc"], ["betat"], scale=-1.0)
            ts("dve", betat, betat, 1.0, None, ALU.add, None, ["betat"], ["betat"])
            S.op("dve", lambda e: e.reciprocal(out=betat, in_=betat), ["betat"], ["betat"])
            tt("dve", gt, ba[:, :, 16:32], dtb.unsqueeze(1).broadcast_to([128, 16, 16]), ALU.add, ["cacc", "dtb"], ["gt"])
            act(gt, gt, AF.Exp, ["gt"], ["gt"])
            act(gt, gt, AF.Ln, ["gt"], ["gt"], bias=1.0)
            tt("dve", gt, gt, negA.unsqueeze(1).broadcast_to([128, 16, 16]), ALU.mult, ["gt", "negA"], ["gt"])

            NU = 16
            qTg = [AR.alloc([HG, 512], BF16) for _ in range(2)]
            kTg = [AR.alloc([HG, 512], BF16) for _ in range(2)]
            vtm = [AR.alloc([4, 512], BF16) for _ in range(2)]
            zw = [AR.alloc([4, 512], BF16) for _ in range(2)]
            upad = AR.alloc([3 + 512], BF16)
            halo = AR.alloc([12, 3], BF16)
            sact = AR.alloc([512], BF16)
            sqb = AR.alloc([512], BF16)
            zs = cacc
            vsb = AR.alloc([512], BF16)
            dg = AR.alloc([4, 128], BF16)
            yTs = [AR.alloc([4, 128], BF16) for _ in range(2)]
            S32 = AR.alloc([HG, 128])
            Sbf = AR.alloc([HG, 128], BF16)
            B2 = AR.alloc([HG, 128])
            WT = AR.alloc([HG, 128], BF16)
            WTb = AR.alloc([HG, 128], BF16)
            NI = 4
            ATl = [AR.alloc([HG, 128], BF16) for _ in range(NI)]
            Tl = [AR.alloc([HG, 128], BF16) for _ in range(NI)]
            NXl = [AR.alloc([HG, 128], BF16) for _ in range(NI)]
            TTl = [AR.alloc([HG, 128], BF16) for _ in range(NI)]
            qkmb = [AR.alloc([HG, 128], BF16) for _ in range(NI)]
            ktmp = [AR.alloc([HG, 128], BF16) for _ in range(NI)]
            Q2Tl = [[AR.alloc([HG, 128], BF16) for _ in range(NI)] for _ in range(2)]
            M2Tl = [[AR.alloc([HG, 128], BF16) for _ in range(NI)] for _ in range(2)]
            WTib = AR.alloc([HG, 128], BF16)
            tmpf = AR.alloc([HG, 128])
            r0 = AR.alloc([HG, 128], BF16)
            o1 = AR.alloc([HG, 128])
            osq = tmpf
            ygd = AR.alloc([HG, 128], BF16)
            cums = AR.alloc([4, 4])
            ecum = [AR.alloc([4, 4]) for _ in range(2)]
            elc = AR.alloc([4, 4])
            elcb = AR.alloc([4, 4])
            glast = [AR.alloc([4, 4]) for _ in range(2)]
            sm = AR.alloc([8])
            ev = set()
            bc4 = lambda a2: a2.unsqueeze(2).broadcast_to([128, HG, 128])
            fl = lambda a3: a3.rearrange("p a b -> p (a b)")
            r4 = lambda a2: a2.rearrange("p (a b) -> p a b", a=HG)

            def gdn_prep():
                for u in range(NU):
                    gi, qb = divmod(u, 4)
                    h0 = gi * HG
                    st = u % 2
                    while u >= 2 and ("Cdone", u - 2) not in ev:
                        yield
                    for kind, cbase in (("q", C_GDQ), ("k", C_GDK), ("v", C_GDV)):
                        b, keys = load_slab([("in", cbase + h0 * 128, 512)])
                        for j in range(HG):
                            cch = (cbase - C_GDQ) // 128 + h0 + j
                            hidx = ("qkv".index(kind)) * HG + j
                            if qb == 0:
                                memset("pool", upad[:, 0:3], 0.0, ["upad"])
                            else:
                                cp("pool", upad[:, 0:3], halo[:, hidx, :], ["halo"], ["upad"])
                            for jj in range(4):
                                ts("dve", dg[:, jj, :], identb[:], cw[:, jj * 48 + cch:jj * 48 + cch + 1], None, ALU.mult, None,
                                   ["identb", "cw"], ["dg"])
                            (pv, pk) = nb("P")
                            for kc in range(16):
                                mm(pv, slab[b][:, kc, j * 128:(j + 1) * 128], hT[:, kc, qb * 512:(qb + 1) * 512],
                                   kc == 0, kc == 15, list(keys) + ["hT"], [pk])
                                if kc % 8 == 7:
                                    yield
                            cp("act", upad[:, 3:515], pv, [pk], ["upad"])
                            if qb < 3:
                                cp("pool", halo[:, hidx, :], upad[:, 512:515], ["upad"], ["halo"])
                            yield
                            (pcv, kcv) = nb("P")
                            for jj in range(4):
                                mm(pcv, dg[:, jj, :], upad[:, jj:jj + 512], jj == 0, jj == 3, ["dg", "upad"], [kcv])
                            yield
                            if kind == "v":
                                act(vsb, pcv, AF.Silu, [kcv], ["vsb"])
                                yield
                                yield
                                (p1, k1) = nb("P")
                                p1b = p1.bitcast(BF16)
                                for tq in range(4):
                                    tr(p1b[:, tq * 128:(tq + 1) * 128], vsb[:, tq * 128:(tq + 1) * 128], identb[:], ["vsb", "identb"], [k1])
                                yield
                                cp("act", vtm[st][:, :, j * 128:(j + 1) * 128],
                                   p1b[:, 0:512].rearrange("p (a b) -> p a b", a=4), [k1], [f"vtm{st}"])
                            else:
                                act(sact, pcv, AF.Silu, [kcv], ["sact"])
                                act(sqb, sact, AF.Square, ["sact"], ["sqb"])
                                yield
                                yield
                                (p1, k1) = nb("P")
                                mm(p1, onesb[:], sqb, True, True, ["sqb", "onesb"], [k1])
                                yield
                                act(lt, p1, AF.Ln, [k1], ["lt"], bias=NORM_EPS)
                                if kind == "q":
                                    act(lt, lt, AF.Exp, ["lt"], ["lt"], scale=-0.5, bias=-0.5 * float(np.log(128.0)))
                                    tt("pool", qTg[st][:, j, :], sact, lt, ALU.mult, ["sact", "lt"], [f"qTg{st}"])
                                else:
                                    act(lt, lt, AF.Exp, ["lt"], ["lt"], scale=-0.5)
                                    tt("pool", kTg[st][:, j, :], sact, lt, ALU.mult, ["sact", "lt"], [f"kTg{st}"])
                            yield
                    b, keys = load_slab([("in", C_GDZ + h0 * 128, 512)])
                    for tl in range(4):
                        (pv, pk) = nb("P")
                        for kc in range(16):
                            mm(pv, hT[:, kc, (qb * 4 + tl) * 128:(qb * 4 + tl + 1) * 128], slab[b][:, kc, 0:512],
                               kc == 0, kc == 15, list(keys) + ["hT"], [pk])
                            if kc % 8 == 7:
                                yield
                        act(zs, pv, AF.Silu, [pk], ["cacc"])
                        tt("pool", r4(zw[st][:, tl, :]), r4(zs), gdw.unsqueeze(1).broadcast_to([128, HG, 128]), ALU.mult,
                           ["cacc", "gdw"], [f"zw{st}"])
                        yield
                    ev.add(("ready", u))

            def gdn_B():
                for u in range(NU):
                    gi, qb = divmod(u, 4)
                    h0 = gi * HG
                    st = u % 2
                    while ("ready", u) not in ev or (u >= 2 and ("Cdone", u - 2) not in ev):
                        yield
                    (pc, kc_) = nb("B")
                    for tl in range(4):
                        mm(pc[:, tl * 4:(tl + 1) * 4], triU, gt[:, qb * 4 + tl, h0:h0 + HG], True, True, ["gt", "cst"], [kc_])
                        mm(pc[:, 64 + tl * 4:64 + (tl + 1) * 4], onesf, gt[:, qb * 4 + tl, h0:h0 + HG], True, True, ["gt", "cst"], [kc_])
                    yield
                    cp("dve", cums, pc[:, 0:16].rearrange("p (a b) -> p a b", a=4), [kc_], ["cums"])
                    act(ecum[st], cums, AF.Exp, ["cums"], [f"ecum{st}"])
                    act(glast[st], pc[:, 64:80].rearrange("p (a b) -> p a b", a=4), AF.Exp, [kc_], [f"glast{st}"])
                    tt("dve", elc, pc[:, 64:80].rearrange("p (a b) -> p a b", a=4), cums, ALU.subtract, [kc_, "cums"], ["elc"])
                    act(elc, elc, AF.Exp, ["elc"], ["elc"])
                    tt("pool", elcb, elc, betat[:, qb * 4:(qb + 1) * 4, h0:h0 + HG], ALU.mult, ["elc", "betat"], ["elcb"])
                    for i in range(NI):
                        t = qb * 4 + i
                        tsl = slice(i * 128, (i + 1) * 128)
                        tt("dve", B2, triU.unsqueeze(1).broadcast_to([128, HG, 128]), bc4(gt[:, t, h0:h0 + HG]), ALU.mult,
                           ["gt", "cst"], ["B2"])
                        yield
                        (p1, k1) = nb("B")
                        mm(p1, maskS, fl(B2), True, False, ["B2", "cst"], [k1])
                        mm(p1, identb[:], fl(negmt4[:]), False, True, ["identb", "negmt4"], [k1])
                        act(fl(WT), p1, AF.Exp, [k1], ["WT"])
                        tt("dve", WTib, WT, i4[:], ALU.add, ["WT", "i4"], ["WTib"])
                        tt("dve", WTb, WT, bc4(betat[:, t, h0:h0 + HG]), ALU.mult, ["WT", "betat"], ["WTb"])
                        tt("dve", WTib, WTib, bc4(betat[:, t, h0:h0 + HG]), ALU.mult, ["WTib", "betat"], ["WTib"])
                        (p2, k2) = nb("B")
                        (p3, k3) = nb("B")
                        for j in range(HG):
                            mm(p2[:, j * 128:(j + 1) * 128], kTg[st][:, j, tsl], kTg[st][:, j, tsl], True, True, [f"kTg{st}"], [k2])
                        for j in range(HG):
                            mm(p3[:, j * 128:(j + 1) * 128], kTg[st][:, j, tsl], qTg[st][:, j, tsl], True, True, [f"kTg{st}", f"qTg{st}"], [k3])
                        yield
                        tt("dve", fl(ATl[i]), p2, fl(WTb), ALU.mult, [k2, "WTb"], [f"AT{i}"])
                        tt("dve", fl(qkmb[i]), p3, fl(WTib), ALU.mult, [k3, "WTib"], [f"qkmb{i}"])
                        (p4, k4) = nb("B")
                        p4b = p4.bitcast(BF16)
                        for j in range(HG):
                            tr(p4b[:, j * 128:(j + 1) * 128], kTg[st][:, j, tsl], identb[:], [f"kTg{st}", "identb"], [k4])
                        tt("dve", ktmp[i], r4(p4b[:, 0:512]), bc4(elcb[:, i, :]), ALU.mult, [k4, "elcb"], [f"ktmp{i}"])
                        yield
                    for lv in range(7):
                        nm = cst[:, K_NEG + lv * 128:K_NEG + (lv + 1) * 128].unsqueeze(1).broadcast_to([128, HG, 128])
                        pxs = []
                        for i in range(NI):
                            Tc, Tk = (i4, "i4") if lv == 0 else (Tl[i], f"T{i}")
                            (px, kx) = nb("B")
                            for j in range(HG):
                                mm(px[:, j * 128:(j + 1) * 128], ATl[i][:, j, :], Tc[:, j, :], True, True, [f"AT{i}", Tk], [kx])
                            pxs.append((px, kx))
                        yield
                        for i in range(NI):
                            (px, kx) = pxs[i]
                            tt("dve", NXl[i], r4(px), nm, ALU.mult, [kx, "cst"], [f"NX{i}"])
                        yield
                        pys = []
                        for i in range(NI):
                            Tc, Tk = (i4, "i4") if lv == 0 else (Tl[i], f"T{i}")
                            TTc, TTk = (i4, "i4") if lv == 0 else (TTl[i], f"TT{i}")
                            (py, ky) = nb("B")
                            for j in range(HG):
                                mm(py[:, j * 128:(j + 1) * 128], TTc[:, j, :], NXl[i][:, j, :], True, True, [TTk, f"NX{i}"], [ky])
                            pys.append((py, ky))
                        yield
                        pzs = []
                        for i in range(NI):
                            (py, ky) = pys[i]
                            Tc, Tk = (i4, "i4") if lv == 0 else (Tl[i], f"T{i}")
                            tt("dve", fl(Tl[i]), py, fl(Tc[:]), ALU.add, [ky, Tk], [f"T{i}"])
                        if lv < 6:
                            for i in range(NI):
                                TTc, TTk = (i4, "i4") if lv == 0 else (TTl[i], f"TT{i}")
                                (pz, kz) = nb("B")
                                for j in range(HG):
                                    mm(pz[:, j * 128:(j + 1) * 128], NXl[i][:, j, :], TTc[:, j, :], True, True, [f"NX{i}", TTk], [kz])
                                pzs.append((pz, kz))
                            yield
                            for i in range(NI):
                                (pz, kz) = pzs[i]
                                TTc, TTk = (i4, "i4") if lv == 0 else (TTl[i], f"TT{i}")
                                tt("dve", fl(TTl[i]), pz, fl(TTc[:]), ALU.add, [kz, TTk], [f"TT{i}"])
                        yield
                    pqs = []
                    for i in range(NI):
                        (pq, kq) = nb("B")
                        for j in range(HG):
                            mm(pq[:, j * 128:(j + 1) * 128], Tl[i][:, j, :], qkmb[i][:, j, :], True, True, [f"T{i}", f"qkmb{i}"], [kq])
                        pqs.append((pq, kq))
                    yield
                    pms = []
                    for i in range(NI):
                        (pq, kq) = pqs[i]
                        cp("act", fl(Q2Tl[st][i]), pq, [kq], [f"Q2T{st}{i}"])
                    for i in range(NI):
                        (pm, km) = nb("B")
                        for j in range(HG):
                            mm(pm[:, j * 128:(j + 1) * 128], Tl[i][:, j, :], ktmp[i][:, j, :], True, True, [f"T{i}", f"ktmp{i}"], [km])
                        pms.append((pm, km))
                    yield
                    for i in range(NI):
                        (pm, km) = pms[i]
                        cp("dve", fl(M2Tl[st][i]), pm, [km], [f"M2T{st}{i}"])
                    yield
                    ev.add(("Bdone", u))

            def gdn_C():
                for u in range(NU):
                    gi, qb = divmod(u, 4)
                    h0 = gi * HG
                    st = u % 2
                    while ("Bdone", u) not in ev:
                        yield
                    if qb == 0:
                        memset("pool", S32, 0.0, ["gS32"])
                        memset("pool", Sbf, 0.0, ["gSbf"])
                    yield
                    for i in range(NI):
                        tl = i
                        t = qb * 4 + i
                        tsl = slice(i * 128, (i + 1) * 128)
                        (p5, k5) = nb("C")
                        (p6, k6) = nb("C")
                        for j in range(HG):
                            mm(p5[:, j * 128:(j + 1) * 128], kTg[st][:, j, tsl], Sbf[:, j, :], True, True, [f"kTg{st}", "gSbf"], [k5])
                        for j in range(HG):
                            mm(p6[:, j * 128:(j + 1) * 128], qTg[st][:, j, tsl], Sbf[:, j, :], True, True, [f"qTg{st}", "gSbf"], [k6])
                        tt("dve", tmpf, r4(p5), bc4(ecum[st][:, tl, :]), ALU.mult, [k5, f"ecum{st}"], ["tmpf"])
                        tt("dve", r0, r4(vtm[st][:, tl, :]), tmpf, ALU.subtract, [f"vtm{st}", "tmpf"], ["r0"])
                        tt("dve", o1, r4(p6), bc4(ecum[st][:, tl, :]), ALU.mult, [k6, f"ecum{st}"], ["o1"])
                        yield
                        (p8, k8) = nb("C")
                        (p9, k9) = nb("C")
                        for j in range(HG):
                            mm(p9[:, j * 128:(j + 1) * 128], M2Tl[st][i][:, j, :], r0[:, j, :], True, True, [f"M2T{st}{i}", "r0"], [k9])
                        for j in range(HG):
                            mm(p8[:, j * 128:(j + 1) * 128], Q2Tl[st][i][:, j, :], r0[:, j, :], True, True, [f"Q2T{st}{i}", "r0"], [k8])
                        yield
                        tt("dve", S32, S32, bc4(glast[st][:, tl, :]), ALU.mult, ["gS32", f"glast{st}"], ["gS32"])
                        tt("dve", S32, r4(p9), S32, ALU.add, [k9, "gS32"], ["gS32"])
                        cp("dve", Sbf, S32, ["gS32"], ["gSbf"])
                        tt("dve", o1, r4(p8), o1, ALU.add, [k8, "o1"], ["o1"])
                        yield
                        act(osq, o1, AF.Square, ["o1", "tmpf"], ["tmpf"])
                        S.op("dve", lambda e: e.tensor_reduce(out=sm[:, 0:4], in_=osq, axis=AX.X, op=ALU.add), ["tmpf"], ["ssq4"])
                        act(sm[:, 4:8], sm[:, 0:4], AF.Ln, ["ssq4"], ["sm48"], scale=1.0 / 128.0, bias=NORM_EPS)
                        act(sm[:, 4:8], sm[:, 4:8], AF.Exp, ["sm48"], ["sm48"], scale=-0.5)
                        tt("dve", o1, o1, bc4(sm[:, 4:8]), ALU.mult, ["o1", "sm48"], ["o1"])
                        tt("dve", ygd, o1, r4(zw[st][:, tl, :]), ALU.mult, ["o1", f"zw{st}"], ["ygd"])
                        yield
                        yield
                        yield
                        (pt, kt) = nb("C")
                        ptb = pt.bitcast(BF16)
                        for j in range(HG):
                            tr(ptb[:, j * 128:(j + 1) * 128], ygd[:, j, :], identb[:], ["ygd", "identb"], [kt])
                        cp("act", yTs[i % 2], ptb[:, 0:512].rearrange("p (a b) -> p a b", a=4), [kt], [f"gyTs{i % 2}"])
                        S.dma("sp", yTb[h0 * 128:(h0 + HG) * 128, t * 128:(t + 1) * 128].rearrange("(a p) n -> p a n", p=128),
                              yTs[i % 2], reads=[f"gyTs{i % 2}"], writes=[f"yTb{i % 2}"], grp=f"yTb{i % 2}")
                        yield
                    ev.add(("Cdone", u))

            run_streams([gdn_prep(), gdn_B(), gdn_C()])
            S.barrier()
            AR.reset()

        if stage >= 3:
            Y2 = AR.alloc([8192])
            yAB = Y2.bitcast(BF16).rearrange("p (a b c) -> p a b c", a=2, b=16)
            yA = yAB[:, 0]
            yB = yAB[:, 1]
            osb = Y2.rearrange("p (a b) -> p a b", a=4)
            mT = AR.alloc([16, 512], BF16)
            ga4 = AR.alloc([4, 512])
            gb4 = AR.alloc([4, 512])
            xt = AR.alloc([2048])
            gate_b = AR.alloc([2048])
            lng = AR.alloc([2048])
            lnb = AR.alloc([2048])
            stl = [AR.alloc([8]) for _ in range(4)]
            S.dma("sp", gate_b, gate_d[0, :].partition_broadcast(128), writes=["gate_b"], grp="gate_b")
            S.dma("sp", lng, ln_g.partition_broadcast(128), writes=["lng"], grp="lng")
            S.dma("sp", lnb, ln_b.partition_broadcast(128), writes=["lnb"], grp="lnb")
            for tb in range(4):
                S.dma("pool", yA, yTa[:, tb * 512:(tb + 1) * 512].rearrange("(kc p) n -> p kc n", p=128), writes=["Y2"], grp="yA")
                S.dma("pool", yB, yTb[:, tb * 512:(tb + 1) * 512].rearrange("(kc p) n -> p kc n", p=128), writes=["Y2"], grp="yB")
                for g4 in range(4):
                    bs, ks = load_slab([("in", C_MG + g4 * 512, 512)])
                    for q in range(4):
                        pv, pk = inproj_fm(bs, ks, q * 128, 128, tb)
                        act(ga4[:, q, :], pv, AF.Sigmoid, [pk], ["ga4"])
                    bs, ks = load_slab([("in", C_MG + 2048 + g4 * 512, 512)])
                    for q in range(4):
                        pv, pk = inproj_fm(bs, ks, q * 128, 128, tb)
                        act(gb4[:, q, :], pv, AF.Sigmoid, [pk], ["gb4"])
                    bs, ks = load_slab([("br0", g4 * 512, 512)])
                    for q in range(4):
                        (pa_, ka_) = nb()
                        for kc in range(16):
                            mm(pa_, slab[bs][:, kc, q * 128:(q + 1) * 128], yA[:, kc, :], kc == 0, kc == 15, ks + ["Y2"], [ka_])
                        tt("dve", ga4[:, q, :], pa_, ga4[:, q, :], ALU.mult, [ka_, "ga4"], ["ga4"])
                    bs, ks = load_slab([("br1", g4 * 512, 512)])
                    for q in range(4):
                        (pb_, kb_) = nb()
                        for kc in range(16):
                            mm(pb_, slab[bs][:, kc, q * 128:(q + 1) * 128], yB[:, kc, :], kc == 0, kc == 15, ks + ["Y2"], [kb_])
                        tt("dve", gb4[:, q, :], pb_, gb4[:, q, :], ALU.mult, [kb_, "gb4"], ["gb4"])
                        tt("dve", mT[:, g4 * 4 + q, :], ga4[:, q, :], gb4[:, q, :], ALU.add, ["ga4", "gb4"], ["mT"])
                for nbk in range(4):
                    bo, ko = load_slab([("out", nbk * 512, 512)])
                    for tq in range(4):
                        (po, kk) = nb()
                        for kc in range(16):
                            mm(po, mT[:, kc, tq * 128:(tq + 1) * 128], slab[bo][:, kc, :], kc == 0, kc == 15, ko + ["mT"], [kk])
                        tt("dve", osb[:, tq, nbk * 512:(nbk + 1) * 512], po, gate_b[:, nbk * 512:(nbk + 1) * 512], ALU.mult,
                           [kk, "gate_b", "Y2"], [("osb", tq)])
                mTf = mT.rearrange("p a b -> p (a b)").bitcast(F32)

                def ln_tile(tq, xb, xk, dump, dk):
                    t = tb * 4 + tq
                    OK = ("osb", tq)
                    S.dma("act", xb, x[t * 128:(t + 1) * 128, :], writes=[xk], grp="x" + xk)
                    r = osb[:, tq, :]
                    yield
                    stt("dve", r, xb, ALPHA, r, ALU.mult, ALU.add, [xk, "Y2", OK], [OK])
                    yield
                    act(dump, r, AF.Identity, [OK], [dk, ("st0", tq)], accum_out=stl[tq][:, 0:1])
                    yield
                    ts("dve", stl[tq][:, 1:2], stl[tq][:, 0:1], -1.0 / D, None, ALU.mult, None, [("st0", tq)], [("st1", tq)])
                    yield
                    act(r, r, AF.Identity, [OK, ("st1", tq)], [OK], bias=stl[tq][:, 1:2])
                    act(dump, r, AF.Square, [OK], [dk, ("st2", tq)], accum_out=stl[tq][:, 2:3])
                    act(stl[tq][:, 3:4], stl[tq][:, 2:3], AF.Ln, [("st2", tq)], [("st3", tq)], scale=1.0 / D, bias=LN_EPS)
                    act(stl[tq][:, 4:5], stl[tq][:, 3:4], AF.Exp, [("st3", tq)], [("st4", tq)], scale=-0.5)
                    yield
                    stt("dve", r, r, stl[tq][:, 4:5], lng, ALU.mult, ALU.mult, [OK, ("st4", tq), "lng"], [OK])
                    yield
                    tt("dve", r, r, lnb, ALU.add, [OK, "lnb"], [OK])
                    yield
                    S.dma("act", out[t * 128:(t + 1) * 128, :], r, reads=[OK], writes=[("OUT", tq)], grp=f"out{tq}", guard=["Y2"])

                ga4f = ga4.rearrange("p a b -> p (a b)")
                gb4f = gb4.rearrange("p a b -> p (a b)")
                run_streams([ln_tile(0, xt, "xt", xt, "xt"), ln_tile(1, mTf[:, 0:2048], "mT", mTf[:, 2048:4096], "mTd")])
                run_streams([ln_tile(2, xt, "xt", xt, "xt"), ln_tile(3, mTf[:, 0:2048], "mT", mTf[:, 2048:4096], "mTd")])
                S.op("pool", lambda e: e.nop(), ["mTd"], ["mT"])
                S.op("pool", lambda e: e.nop(), [("osb", q_) for q_ in range(4)] + [("OUT", q_) for q_ in range(4)], ["Y2"])
        S.barrier()
        S.emit(final_wait_keys=[])
        build.stats = dict(nops=len(S.ops), nsem=S.nsem, maxval=S.maxval, arena=AR.off)
    return nc


_NC_CACHE = {}


def kernel(**inputs):
    stage = 9
    if stage not in _NC_CACHE:
        _NC_CACHE[stage] = build(stage)
    nc = _NC_CACHE[stage]
    cst = host_consts()
    f = lambda a: np.ascontiguousarray(np.asarray(a, dtype=np.float32))
    shared = dict(
        w_ada=f(inputs["w_ada"][0]), b_ada=f(inputs["b_ada"][0]), w_in=f(inputs["w_in"][0]),
        w_gk2=f(inputs["w_gk2"][0]), b_gk2=f(inputs["b_gk2"][0]), conv_w=f(inputs["conv_w"][0]),
        a_log=f(inputs["a_log"][0]), dt_bias=f(inputs["dt_bias"][0]),
        gla_norm_w=f(inputs["gla_norm_w"][0]), gdn_norm_w=f(inputs["gdn_norm_w"][0]),
        w_branch=f(inputs["w_branch"][0]), w_out=f(inputs["w_out"][0]),
        ln_g=f(inputs["ln_g"][0]), ln_b=f(inputs["ln_b"][0]), cst=cst)
    x = np.asarray(inputs["x"], dtype=np.float32)
    c = np.asarray(inputs["c"], dtype=np.float32)
    in_maps = []
    for b in range(8):
        m = dict(shared)
        m["x"] = np.ascontiguousarray(x[b])
        m["c"] = np.ascontiguousarray(c[b])
        in_maps.append(m)
    res = run_bass_kernel_spmd(nc, in_maps, core_ids=list(range(8)))
    return np.stack([np.asarray(r["out"], dtype=np.float32) for r in res.results], axis=0)
```
